# Optimizing a Trainium2 kernel written in Bass

```python
import jax, jax.numpy as jnp
from jax import lax
import numpy as np

D_MODEL = 1024
BATCH = 4
SEQ = 4096
DEPTH = 2
DEC_BATCH = 8
DEC_SEQ = 16
PAST_LEN = 2048

CHUNK = 64
N_A = DEPTH // 2
MIX_W = D_MODEL // 2
POOL_WINDOWS = (2, 4, 8, 16)
POOL_GROUP = MIX_W // len(POOL_WINDOWS)
POOL_BUF = max(POOL_WINDOWS) - 1
FOX_HEADS = 8
FOX_HD = MIX_W // FOX_HEADS
MEM_HEADS = 4
MEM_W = D_MODEL - MIX_W
MEM_HD = MEM_W // MEM_HEADS
N_MEM = 256
D_FF = 4 * D_MODEL
Q_BLOCK = 128
EPS = 1e-6
FORGET_BIAS = 3.0

kernel_name = "yoco_pool_fox_mem_streaming_step"


def rmsnorm(x, g):
    xf = x.astype(jnp.float32)
    y = xf * lax.rsqrt(jnp.mean(xf * xf, axis=-1, keepdims=True) + EPS) * g.astype(jnp.float32)
    return y.astype(x.dtype)


def pool_mixer(u, u_prev, w_pool, scale):
    B, T, _ = u.shape
    P = u_prev.shape[1]
    ext = jnp.concatenate([u_prev.astype(u.dtype), u], axis=1)
    cs = jnp.cumsum(ext.astype(jnp.float32), axis=1)
    cs = jnp.concatenate([jnp.zeros((B, 1, MIX_W), jnp.float32), cs], axis=1)
    hi = np.arange(P, P + T) + 1
    outs = []
    for g, w in enumerate(POOL_WINDOWS):
        lo = np.maximum(hi - w, 0)
        cnt = jnp.asarray((hi - lo).astype(np.float32))[None, :, None]
        csg = cs[..., g * POOL_GROUP:(g + 1) * POOL_GROUP]
        mean = (csg[:, hi] - csg[:, lo]) / cnt
        outs.append(mean - u[..., g * POOL_GROUP:(g + 1) * POOL_GROUP].astype(jnp.float32))
    pooled = jnp.stack(outs, axis=2).astype(u.dtype)
    y = jnp.einsum('btgc,gcd->btgd', pooled, w_pool).reshape(B, T, MIX_W) * scale
    return y, ext[:, -POOL_BUF:]


def shared_kvf(x, g_kv, w_kvf, b_f):
    B, T, _ = x.shape
    kvf = rmsnorm(x, g_kv) @ w_kvf
    k = kvf[..., :MIX_W].reshape(B, T, FOX_HEADS, FOX_HD)
    v = kvf[..., MIX_W:2 * MIX_W].reshape(B, T, FOX_HEADS, FOX_HD)
    logf = jax.nn.log_sigmoid(kvf[..., 2 * MIX_W:].astype(jnp.float32) + b_f.astype(jnp.float32))
    return k, v, logf


def fox_attention(q, k_all, v_all, logf_all, P):
    B, T = q.shape[:2]
    L = k_all.shape[1]
    C = jnp.cumsum(logf_all.astype(jnp.float32), axis=1)
    ck = jnp.transpose(C, (0, 2, 1))[:, :, None, :]
    cq = C[:, P:]
    kpos = jnp.arange(L)
    qpos = P + jnp.arange(T)
    scale = FOX_HD ** -0.5

    def block(args):
        qb, cqb, qpb = args
        s = jnp.einsum('bqhd,bkhd->bhqk', qb, k_all, preferred_element_type=jnp.float32) * scale
        s = s + jnp.transpose(cqb, (0, 2, 1))[..., None] - ck
        s = jnp.where(kpos[None, None, None, :] <= qpb[None, None, :, None], s, -jnp.inf)
        p = jax.nn.softmax(s, axis=-1).astype(v_all.dtype)
        return jnp.einsum('bhqk,bkhd->bqhd', p, v_all)

    if T % Q_BLOCK == 0:
        nb = T // Q_BLOCK
        qb = q.reshape(B, nb, Q_BLOCK, FOX_HEADS, FOX_HD).transpose(1, 0, 2, 3, 4)
        cqb = cq.reshape(B, nb, Q_BLOCK, FOX_HEADS).transpose(1, 0, 2, 3)
        qpb = qpos.reshape(nb, Q_BLOCK)
        out = lax.map(block, (qb, cqb, qpb))
        out = out.transpose(1, 0, 2, 3, 4)
    else:
        out = block((q, cq, qpos))
    return out.reshape(B, T, MIX_W)


def mem_project(mem, g_mem_l, w_mem_kv_l):
    B, N, _ = mem.shape
    kv = rmsnorm(mem, g_mem_l) @ w_mem_kv_l
    mk = kv[..., :MEM_W].reshape(B, N, MEM_HEADS, MEM_HD)
    mv = kv[..., MEM_W:].reshape(B, N, MEM_HEADS, MEM_HD)
    return mk, mv


def mem_attention(q, mk, mv):
    B, T, _ = q.shape
    qh = q.reshape(B, T, MEM_HEADS, MEM_HD)
    s = jnp.einsum('bqhd,bkhd->bhqk', qh, mk, preferred_element_type=jnp.float32) * (MEM_HD ** -0.5)
    p = jax.nn.softmax(s, axis=-1).astype(mv.dtype)
    return jnp.einsum('bhqk,bkhd->bqhd', p, mv).reshape(B, T, MEM_W)


def squared_relu_mlp(x, g_pre, g_post, w_up, w_down):
    a = jnp.square(jax.nn.relu(rmsnorm(x, g_pre) @ w_up))
    return rmsnorm(a @ w_down, g_post)


def run_trunk(x, pool_prev, k_prev, v_prev, logf_prev, mem_k, mem_v,
              g_mix_pre, g_mix_post, g_mlp_pre, g_mlp_post, w_in, w_out, w_pool, pool_scale,
              g_kv, w_kvf, b_f, w_up, w_down):
    P_kv = k_prev.shape[1]
    pool_states = []
    k_new = v_new = logf_new = None
    k_all = v_all = logf_all = None
    for l in range(DEPTH):
        h = rmsnorm(x, g_mix_pre[l])
        proj = h @ w_in[l]
        mix_in, mem_q = proj[..., :MIX_W], proj[..., MIX_W:]
        if l < N_A:
            mix_out, st = pool_mixer(mix_in, pool_prev[l], w_pool[l], pool_scale[l])
            pool_states.append(st)
        else:
            if k_new is None:
                k_new, v_new, logf_new = shared_kvf(x, g_kv, w_kvf, b_f)
                k_all = jnp.concatenate([k_prev.astype(k_new.dtype), k_new], axis=1)
                v_all = jnp.concatenate([v_prev.astype(v_new.dtype), v_new], axis=1)
                logf_all = jnp.concatenate([logf_prev.astype(jnp.float32), logf_new], axis=1)
            B, T, _ = mix_in.shape
            q = mix_in.reshape(B, T, FOX_HEADS, FOX_HD)
            mix_out = fox_attention(q, k_all, v_all, logf_all, P_kv)
        mem_out = mem_attention(mem_q, mem_k[l], mem_v[l])
        x = x + rmsnorm(jnp.concatenate([mix_out.astype(x.dtype), mem_out.astype(x.dtype)], axis=-1) @ w_out[l],
                        g_mix_post[l])
        x = x + squared_relu_mlp(x, g_mlp_pre[l], g_mlp_post[l], w_up[l], w_down[l])
    return x, jnp.stack(pool_states, axis=0), k_new, v_new, logf_new


def setup_inputs(seed: int = 0) -> dict:
    key = jax.random.key(seed)
    ks = jax.random.split(key, 32)

    def nrm(k, shape, scale=1.0):
        return jax.random.normal(k, shape, jnp.float32) * scale

    def gain(k, shape):
        return 1.0 + 0.05 * jax.random.normal(k, shape, jnp.float32)

    w_kvf = jnp.concatenate([nrm(ks[20], (D_MODEL, 2 * MIX_W), D_MODEL ** -0.5),
                             nrm(ks[21], (D_MODEL, FOX_HEADS), 0.5 * D_MODEL ** -0.5)], axis=1)
    return {
        "x_prompt": nrm(ks[0], (BATCH, SEQ, D_MODEL)),
        "x_sample": nrm(ks[1], (DEC_BATCH, DEC_SEQ, D_MODEL)),
        "cache_pool": nrm(ks[2], (N_A, DEC_BATCH, POOL_BUF, MIX_W)),
        "cache_k": nrm(ks[3], (DEC_BATCH, PAST_LEN, FOX_HEADS, FOX_HD)),
        "cache_v": nrm(ks[4], (DEC_BATCH, PAST_LEN, FOX_HEADS, FOX_HD)),
        "cache_logf": jax.nn.log_sigmoid(FORGET_BIAS + nrm(ks[5], (DEC_BATCH, PAST_LEN, FOX_HEADS), 0.5)),
        "cache_mem_k": nrm(ks[6], (DEPTH, DEC_BATCH, N_MEM, MEM_HEADS, MEM_HD)),
        "cache_mem_v": nrm(ks[7], (DEPTH, DEC_BATCH, N_MEM, MEM_HEADS, MEM_HD)),
        "mem_prompt": nrm(ks[8], (BATCH, N_MEM, D_MODEL)),
        "g_mix_pre": gain(ks[9], (DEPTH, D_MODEL)),
        "g_mix_post": gain(ks[10], (DEPTH, D_MODEL)),
        "g_mlp_pre": gain(ks[11], (DEPTH, D_MODEL)),
        "g_mlp_post": gain(ks[12], (DEPTH, D_MODEL)),
        "w_in": nrm(ks[13], (DEPTH, D_MODEL, MIX_W + MEM_W), D_MODEL ** -0.5),
        "w_out": nrm(ks[14], (DEPTH, MIX_W + MEM_W, D_MODEL), (MIX_W + MEM_W) ** -0.5),
        "w_pool": nrm(ks[15], (N_A, len(POOL_WINDOWS), POOL_GROUP, POOL_GROUP), POOL_GROUP ** -0.5),
        "pool_scale": 1.0 + 0.1 * nrm(ks[16], (N_A, MIX_W)),
        "g_kv": gain(ks[17], (D_MODEL,)),
        "w_kvf": w_kvf,
        "b_f": FORGET_BIAS + 0.5 * nrm(ks[18], (FOX_HEADS,)),
        "g_mem": gain(ks[19], (DEPTH, D_MODEL)),
        "w_mem_kv": nrm(ks[22], (DEPTH, D_MODEL, 2 * MEM_W), D_MODEL ** -0.5),
        "w_up": nrm(ks[23], (DEPTH, D_MODEL, D_FF), D_MODEL ** -0.5),
        "w_down": nrm(ks[24], (DEPTH, D_FF, D_MODEL), D_FF ** -0.5),
    }


def reference(x_prompt, x_sample, cache_pool, cache_k, cache_v, cache_logf, cache_mem_k, cache_mem_v,
              mem_prompt, g_mix_pre, g_mix_post, g_mlp_pre, g_mlp_post, w_in, w_out, w_pool, pool_scale,
              g_kv, w_kvf, b_f, g_mem, w_mem_kv, w_up, w_down):
    mks, mvs = [], []
    for l in range(DEPTH):
        mk, mv = mem_project(mem_prompt, g_mem[l], w_mem_kv[l])
        mks.append(mk)
        mvs.append(mv)
    mem_k_prompt = jnp.stack(mks, axis=0)
    mem_v_prompt = jnp.stack(mvs, axis=0)

    B = x_prompt.shape[0]
    dt = x_prompt.dtype
    no_pool = jnp.zeros((N_A, B, 0, MIX_W), dt)
    no_k = jnp.zeros((B, 0, FOX_HEADS, FOX_HD), dt)
    no_logf = jnp.zeros((B, 0, FOX_HEADS), jnp.float32)

    y_prompt, pool_state_prompt, k_prompt, v_prompt, logf_prompt = run_trunk(
        x_prompt, no_pool, no_k, no_k, no_logf, mem_k_prompt, mem_v_prompt,
        g_mix_pre, g_mix_post, g_mlp_pre, g_mlp_post, w_in, w_out, w_pool, pool_scale,
        g_kv, w_kvf, b_f, w_up, w_down)

    y_sample, pool_state_sample, k_sample, v_sample, logf_sample = run_trunk(
        x_sample, cache_pool, cache_k, cache_v, cache_logf, cache_mem_k, cache_mem_v,
        g_mix_pre, g_mix_post, g_mlp_pre, g_mlp_post, w_in, w_out, w_pool, pool_scale,
        g_kv, w_kvf, b_f, w_up, w_down)

    return (y_prompt, y_sample, pool_state_prompt, pool_state_sample,
            k_prompt, v_prompt, logf_prompt, k_sample, v_sample, logf_sample,
            mem_k_prompt, mem_v_prompt)
```

```python
import os
from contextlib import ExitStack

import numpy as np
import ml_dtypes
import concourse.bass as bass
import concourse.mybir as mybir
from concourse.bass_utils import run_bass_kernel_spmd

F32 = mybir.dt.float32
BF16 = mybir.dt.bfloat16
I32 = mybir.dt.int32
AF = mybir.ActivationFunctionType
ALU = mybir.AluOpType

D = 1024
KC = 8
SEQ = 4096
TT = 512
NT0 = SEQ // TT
NT1 = NT0 // 2
NBLK = SEQ // 128
PAST = 2048
NSB = PAST // 128
ST = 16
EPS = 1e-6
NEG = -30000.0

G_MIX_PRE, G_MIX_POST, G_MLP_PRE, G_MLP_POST, G_MEM = 0, 16, 32, 48, 64
G_KV = 80
NG = 88


class Res:
    __slots__ = ("name", "last_write", "reads", "extra")

    def __init__(self, name=""):
        self.name = name
        self.last_write = None
        self.reads = {}
        self.extra = []


class Sched:
    COMPUTE = ("pe", "act", "dve", "pool")

    def __init__(self, nc):
        self.nc = nc
        self.streams = {e: [] for e in ("pe", "act", "dve", "pool", "sp")}
        self.sems = {}
        self.cnt = {}
        self.seen = {e: {} for e in self.streams}
        self.ring = {"sp": 24, "act": 4, "pool": 20}
        self.ring_pos = {q: 0 for q in self.ring}

    def open(self, stack):
        nc = self.nc
        for e in self.COMPUTE:
            self.sems[e] = stack.enter_context(nc.semaphore("s_" + e))
            self.cnt[e] = 0
        for q, n in self.ring.items():
            for i in range(n):
                src = ("dma", q, i)
                self.sems[src] = stack.enter_context(nc.semaphore("d_%s%d" % (q, i)))
                self.cnt[src] = 0

    def _deps(self, reads, writes, merge=False):
        deps = {}

        def add(d):
            if d is not None and d[1] > deps.get(d[0], 0):
                deps[d[0]] = d[1]
        for r in reads:
            add(r.last_write)
            for d in r.extra:
                add(d)
        for w in writes:
            if not merge:
                add(w.last_write)
                for d in w.extra:
                    add(d)
            for s, c in w.reads.items():
                add((s, c))
        return deps

    def _waits(self, eng, deps):
        out = []
        seen = self.seen[eng]
        for s, c in deps.items():
            if c > seen.get(s, 0):
                seen[s] = c
                out.append((self.sems[s], c))
        return out

    def _record(self, src, count, reads, writes, merge=False):
        for r in reads:
            if count > r.reads.get(src, 0):
                r.reads[src] = count
        for w in writes:
            if merge:
                w.extra.append((src, count))
            else:
                w.last_write = (src, count)
                w.extra = []
                w.reads = {}

    def op(self, eng, emit, reads=(), writes=()):
        reads = [r for r in reads if r is not None]
        writes = [w for w in writes if w is not None]
        deps = self._deps(reads, writes)
        if eng in deps and eng == "pe":
            del deps[eng]
        waits = self._waits(eng, deps)
        self.cnt[eng] += 1
        c = self.cnt[eng]
        sem = self.sems[eng]

        def run(e, waits=waits, emit=emit, sem=sem):
            for s, v in waits:
                e.wait_ge(s, v)
            emit(e).then_inc(sem, 1)
        self.streams[eng].append(run)
        self._record(eng, c, reads, writes)

    def dma(self, q, out, in_, reads=(), writes=(), merge=False):
        reads = [r for r in reads if r is not None]
        writes = [w for w in writes if w is not None]
        n = self.ring[q]
        slot = self.ring_pos[q] % n
        self.ring_pos[q] += 1
        src = ("dma", q, slot)
        deps = self._deps(reads, writes, merge)
        if self.cnt[src] > deps.get(src, 0):
            deps[src] = self.cnt[src]
        waits = self._waits(q, deps)
        self.cnt[src] += 16
        c = self.cnt[src]
        sem = self.sems[src]

        def run(e, waits=waits, sem=sem, out=out, in_=in_):
            for s, v in waits:
                e.wait_ge(s, v)
            e.dma_start(out=out, in_=in_).then_inc(sem, 16)
        self.streams[q].append(run)
        self._record(src, c, reads, writes, merge)

    def finish(self, eng="pool"):
        waits = [(self.sems[s], c) for s, c in self.cnt.items() if c > 0 and s != eng]

        def run(e, waits=waits):
            for s, v in waits:
                e.wait_ge(s, v)
        self.streams[eng].append(run)

    def emit_all(self):
        with self.nc.Block() as block:
            @block.tensor
            def _(e):
                for f in self.streams["pe"]:
                    f(e)

            @block.scalar
            def _(e):
                for f in self.streams["act"]:
                    f(e)

            @block.vector
            def _(e):
                for f in self.streams["dve"]:
                    f(e)

            @block.gpsimd
            def _(e):
                for f in self.streams["pool"]:
                    f(e)

            @block.sync
            def _(e):
                for f in self.streams["sp"]:
                    f(e)


class Ring:
    def __init__(self, items):
        self.items = items
        self.i = 0

    def next(self):
        it = self.items[self.i % len(self.items)]
        self.i += 1
        return it


def build(stage=99):
    nc = bass.Bass("TRN2", target_bir_lowering=False)

    def din(name, shape, dt=F32):
        return nc.dram_tensor(name, list(shape), dt, kind="ExternalInput").ap()

    def dout(name, shape, dt=F32):
        return nc.dram_tensor(name, list(shape), dt, kind="ExternalOutput").ap()

    def dint(name, shape, dt=BF16):
        return nc.dram_tensor(name, list(shape), dt, kind="Internal").ap()

    xT = din("xT", [D, SEQ])
    memT = din("memT", [D, 256])
    xsT = din("xsT", [D, ST])
    poolprev = din("poolprev", [512, 15])
    ckT = din("ckT", [512, PAST])
    cv = din("cv", [PAST, 512])
    clf = din("clf", [PAST, 8])
    cmkT = din("cmkT", [2, 512, 256])
    cmv = din("cmv", [2, 256, 512])
    w_in = din("w_in", [2, D, D])
    w_out = din("w_out", [2, D, D])
    w_pool = din("w_pool", [4, 128, 128])
    w_kvf = din("w_kvf", [D, 1032])
    w_mem = din("w_mem", [2, D, D])
    w_up = din("w_up", [2, D, 4096])
    w_down = din("w_down", [2, 4096, D])
    gains = din("gains", [128, NG])
    pscale = din("pscale", [128, 4])
    bfb = din("bfb", [128, 32])
    flag = din("flag", [128, 256], I32)
    masks = din("masks", [8, 128, TT], BF16)
    mask_s = din("mask_s", [128, TT], BF16)
    cntfix = din("cntfix", [128, 4, 16])
    tri = din("tri", [128, 128])
    lastrow = din("lastrow", [128, 128])
    lastrow16 = din("lastrow16", [128, 128])
    pmask = din("pmask", [128, 2])
    rflag = din("rflag", [128, 2])
    ident_in = din("ident", [128, 128], BF16)

    yT_own = dout("yT_own", [D, SEQ // 2])
    ysT = dout("ysT", [D, ST])
    pstate_p = dout("pstate_p", [512, 15])
    pstate_s = dout("pstate_s", [512, 15])
    kT_p = dout("kT_p", [512, SEQ])
    v_p = dout("v_p", [SEQ, 512])
    logf_p = dout("logf_p", [SEQ, 8])
    kT_s = dout("kT_s", [512, ST])
    v_s = dout("v_s", [ST, 512])
    logf_s = dout("logf_s", [ST, 8])
    mkT_p = dout("mkT_p", [2, 512, 256])
    mv_p = dout("mv_p", [2, 256, 512])

    win_bf = dint("win_bf", [2, D, D])
    wout_bf = dint("wout_bf", [2, D, D])
    wpool_bf = dint("wpool_bf", [4, 128, 128])
    wkvf_bf = dint("wkvf_bf", [D, 1032])
    wmem_bf = dint("wmem_bf", [2, D, D])
    wup_bf = dint("wup_bf", [2, D, 4096])
    wdown_bf = dint("wdown_bf", [2, 4096, D])
    kbf = dint("kbf", [512, SEQ])
    vbf = dint("vbf", [SEQ, 520])
    kbf_s = dint("kbf_s", [512, PAST + ST])
    vbf_s = dint("vbf_s", [PAST + ST, 520])
    x1own = dint("x1own", [D, SEQ // 2], F32)
    x1s = dint("x1s", [D, ST], F32)

    S = Sched(nc)
    with ExitStack() as st:
        S.open(st)

        def sb(name, shape, dt):
            return st.enter_context(nc.sbuf_tensor(name, list(shape), dt))

        xbufs = [sb("xbufs[X.i]%d" % i, [128, KC, TT], F32) for i in range(2)]
        r_xs = [[Res("x%d_%d" % (i, c)) for c in range(KC)] for i in range(2)]

        class _X:
            i = 0
        X = _X()
        hbuf = sb("hbuf", [128, KC, TT], BF16);     r_h = [Res("h%d" % c) for c in range(KC)]
        sqatt = sb("sqatt", [128, KC, TT], BF16)
        r_sqc = [Res("sq%d" % c) for c in range(KC)]
        r_att = r_sqc
        if os.environ.get("KPAD"):
            _pad = sb("padz", [128, int(os.environ["KPAD"])], F32)
        zbuf = sb("zbuf", [128, KC, TT], F32);      r_z = [Res("z%d" % c) for c in range(KC)]
        abuf = sb("abuf", [128, 32 * 520], BF16)
        r_a = [Res("a%d" % g) for g in range(8)]
        a_v = abuf[:, 0:32 * TT].rearrange("p (c t) -> p c t", t=TT)
        vaug_v = abuf[:, :].rearrange("p (b h e) -> p b h e", h=8, e=65)
        qm = sb("qm", [128, 4, TT], BF16);          r_qm = [Res("qm%d" % c) for c in range(4)]
        wts = [sb("wt%d" % i, [128, 8, TT], BF16) for i in range(3)]
        wring = Ring([(w, Res("wt%d" % i)) for i, w in enumerate(wts)])
        wf_sb = sb("wf_sb", [128, 8, 8], BF16); r_wf = Res("wf")
        rtmp = Ring([(sb("rt%d" % i, [128, TT], BF16), Res("rt%d" % i)) for i in range(3)])
        ptmp = Ring([(sb("pt%d" % i, [128, TT], BF16), Res("pt%d" % i)) for i in range(4)])
        lnv = sb("lnv", [128, TT], F32);            r_ln = Res("ln")
        rstd = sb("rstd", [128, TT], F32);          r_rstd = Res("rstd")
        rden = sb("rden", [128, TT], F32);          r_rden = Res("rden")
        rstd2 = sb("rstd2", [128, TT], F32);        r_rstd2 = Res("rstd2")
        uni = sb("uni", [128, 8 * TT + 2 * 4096], BF16)
        u_v = uni[:, 0:2 * 4 * 528].bitcast(F32).rearrange("p (c w) -> p c w", w=528)
        o1 = 2 * 4 * 528
        sA = uni[:, o1:o1 + 2 * 528].bitcast(F32)
        sB = uni[:, o1 + 2 * 528:o1 + 4 * 528].bitcast(F32)
        o2 = o1 + 4 * 528
        pooled = uni[:, o2:o2 + 4 * TT].rearrange("p (c t) -> p c t", t=TT)
        r_u = Res("u"); r_sA = Res("sA"); r_sB = Res("sB"); r_pooled = [Res("pl%d" % g) for g in range(4)]
        qaug = uni[:, 0:8 * TT].rearrange("p (h t) -> p h t", t=TT)
        r_qaug = [Res("qa%d" % h) for h in range(8)]
        khs = [uni[:, 8 * TT:8 * TT + 4096], uni[:, 8 * TT + 4096:8 * TT + 8192]]
        khring = Ring([(khs[0], Res("kh0")), (khs[1], Res("kh1"))])
        attnh = sb("attnh", [128, 8, TT], BF16);    r_attnh = [Res("ah%d" % h) for h in range(8)]
        memk = [sb("memk%d" % i, [128, 4, 256], BF16) for i in range(2)]
        memv = [sb("memv%d" % i, [128, 2, 512], BF16) for i in range(2)]
        r_memk = [Res("mk0"), Res("mk1")]
        r_memv = [Res("mv0"), Res("mv1")]
        vf = Ring([(sb("vf%d" % i, [128, 512], F32), Res("vf%d" % i)) for i in range(2)])
        vst = [sb("vst%d" % i, [128, 8, 65], BF16) for i in range(2)]
        vstr = Ring([(vst[0], Res("vst0")), (vst[1], Res("vst1"))])
        lfr = Ring([(sb("lf%d" % i, [128, 32], F32), Res("lf%d" % i)) for i in range(2)])
        fl = sb("fl", [128, 32], F32); r_fl = Res("fl")
        fe = sb("fe", [128, 32], F32); r_fe = Res("fe")
        Call = sb("Call", [128, NBLK, 8], F32);     r_Call = [Res("C%d" % b) for b in range(NBLK)]
        Cb_all = sb("Cb_all", [128, NBLK, 8], F32); r_Cb = [Res("Cb%d" % b) for b in range(NBLK)]
        Cb_own = sb("Cb_own", [128, NBLK // 2, 8], F32); r_Cbo = [Res("Cbo%d" % b) for b in range(NBLK // 2)]
        Call_s = sb("Call_s", [128, NSB + 1, 8], F32); r_Calls = [Res("Cs%d" % b) for b in range(NSB + 1)]
        Cb_s = sb("Cb_s", [128, 8], F32); r_Cbs = Res("Cbs")
        clf_sb = sb("clf_sb", [128, NSB, 8], F32); r_clf = Res("clf")
        zeros8 = sb("zeros8", [128, 8], F32); r_zeros8 = Res("z8")
        biasK = sb("biasK", [128, NBLK, 8], F32); r_biasK = Res("biasK")
        offt = sb("offt", [128, 4, 8], F32); r_offt = Res("offt")
        offhi = sb("offhi", [128, 4, 8], BF16); r_offhi = Res("offhi")
        offmid = sb("offmid", [128, 4, 8], BF16); r_offmid = Res("offmid")
        src2 = sb("src2", [128, 4, 8], F32); r_src2 = Res("src2")
        tmp2 = sb("tmp2", [128, 4, 8], F32); r_tmp2 = Res("tmp2")
        zrow = sb("zrow", [128, 128], F32); r_zrow = Res("zrow")
        xown = abuf[:, 0:8 * TT].bitcast(F32).rearrange("p (c t) -> p c t", t=256)
        ones_bf = sb("ones_bf", [128, 128], BF16)
        onesT = sb("onesT", [128, TT], BF16)
        rflag_sb = sb("rflag_sb", [128, 2], F32)
        onesf = sb("onesf", [128, 128], F32)
        ident = sb("ident_sb", [128, 128], BF16)
        tri_sb = sb("tri_sb", [128, 128], F32)
        lastrow_sb = sb("lastrow_sb", [128, 128], F32)
        lastrow16_sb = sb("lastrow16_sb", [128, 128], F32)
        g32 = sb("g32", [128, NG], F32)
        graw = sb("graw", [128, NG], F32)
        pscale_sb = sb("pscale_sb", [128, 4], F32)
        bfb_sb = sb("bfb_sb", [128, 32], F32)
        flag_sb = sb("flag_sb", [128, 256], I32)
        masks_sb = sb("masks_sb", [128, 8, TT], BF16)
        mask_s_sb = sb("mask_s_sb", [128, TT], BF16)
        cntfix_sb = sb("cntfix_sb", [128, 4, 16], F32)
        pmask_sb = sb("pmask_sb", [128, 2], F32)
        wpool_sb = sb("wpool_sb", [128, 4, 128], BF16)
        r_const = Res("const")
        r_g32 = Res("g32")
        r_wpool = Res("wpool")

        pss = [st.enter_context(nc.psum_tensor("ps%d" % i, [128, 512], F32)) for i in range(8)]
        psring = Ring([(p, Res("ps%d" % i)) for i, p in enumerate(pss[0:4])])
        accring = Ring([(p, Res("acc%d" % i)) for i, p in enumerate(pss[4:8])])

        def mm(out_ap, pairs, reads, writes, first=True, last=True):
            def emit(e, pairs=pairs, out_ap=out_ap):
                n = len(pairs)
                for i, (l, r) in enumerate(pairs):
                    ins = e.matmul(out_ap, lhsT=l, rhs=r, start=(first and i == 0), stop=(last and i == n - 1))
                return ins
            S.op("pe", emit, reads, writes)

        def act(func, out, in_, reads, writes, **kw):
            S.op("act", lambda e: e.activation(out=out, in_=in_, func=func, **kw), reads, writes)

        def act_copy(out, in_, reads, writes):
            if out.dtype == BF16:
                act(AF.Copy, out, in_, reads, writes)
            else:
                np_ = in_.shape[0]
                p0 = 64 if np_ == 1 else 0
                act(AF.Copy, out, in_, reads + [r_const], writes, scale=onesf[p0:p0 + np_, 0:1])

        def dve_tt(out, in0, in1, op, reads, writes, eng="dve"):
            S.op(eng, lambda e: e.tensor_tensor(out=out, in0=in0, in1=in1, op=op), reads, writes)

        def dve_stt(out, in0, scalar, in1, op0, op1, reads, writes):
            S.op("dve", lambda e: e.scalar_tensor_tensor(out=out, in0=in0, scalar=scalar, in1=in1, op0=op0, op1=op1),
                 reads, writes)

        def dve_ts(out, in0, s1, s2, op0, op1, reads, writes, eng="dve"):
            S.op(eng, lambda e: e.tensor_scalar(out=out, in0=in0, scalar1=s1, scalar2=s2, op0=op0, op1=op1),
                 reads, writes)

        def dve_copy(out, in_, reads, writes, eng="dve"):
            if out.dtype == BF16:
                S.op(eng, lambda e: e.tensor_copy(out=out, in_=in_), reads, writes)
            else:
                S.op(eng, lambda e: e.tensor_scalar(out=out, in0=in_, scalar1=1.0, scalar2=None, op0=ALU.mult,
                                                    op1=ALU.bypass), reads, writes)

        def memset(ap, val, writes, eng="dve"):
            S.op(eng, lambda e: e.memset(ap, val), (), writes)

        def load_w(src2d, k0, nk, n0, nn, rsrc, q="sp"):
            wt, rw = wring.next()
            S.dma(q, wt[:, 0:nk, 0:nn],
                  src2d[k0 * 128:(k0 + nk) * 128, n0:n0 + nn].rearrange("(c p) n -> p c n", p=128),
                  reads=rsrc, writes=[rw])
            return wt, rw

        memset(ones_bf[:], 1.0, [r_const])
        memset(onesT[:], 1.0, [r_const])
        memset(onesf[:], 1.0, [r_const])
        memset(zeros8[:], 0.0, [r_zeros8])
        memset(zrow[:], 0.0, [r_zrow])
        memset(Call_s[:, NSB, :], 0.0, [r_Calls[NSB]])
        for v in vst:
            memset(v[:], 1.0, [r_const])
        for (dst, src) in ((graw, gains), (pscale_sb, pscale), (bfb_sb, bfb), (flag_sb, flag), (mask_s_sb, mask_s),
                           (cntfix_sb, cntfix), (tri_sb, tri), (lastrow_sb, lastrow), (lastrow16_sb, lastrow16),
                           (pmask_sb, pmask), (ident, ident_in), (rflag_sb, rflag)):
            S.dma("sp", dst[:], src, writes=[r_const], merge=True)
        S.dma("sp", masks_sb[:], masks.rearrange("m p t -> p m t"), writes=[r_const], merge=True)
        dve_ts(g32[:], graw[:], 32.0, None, ALU.mult, ALU.bypass, [r_const], [r_g32])

        rW = {}
        SK = os.environ.get("KSKIP", "")

        cast_hist = []
        cast_todo = []

        def cast_piece(name, l, dst2d, src2d, extra=()):
            r = Res(name + str(l))
            rW.setdefault((name, l), []).append(r)
            prev = [cast_hist[-2]] if len(cast_hist) >= 2 else []
            S.dma("pool", dst2d, src2d, reads=prev + list(extra), writes=[r])
            cast_hist.append(r)

        def flat(ap, cols):
            return ap.rearrange("r (a n) -> (r a) n", n=cols)

        def cast_w(name, l, dst, src, cols, defer=False, rows_per=512):
            d2, s2 = flat(dst, cols), flat(src, cols)
            if not defer:
                cast_piece(name, l, d2, s2)
                return
            rW.setdefault((name, l), [])
            n = d2.shape[0]
            for r0 in range(0, n, rows_per):
                cast_todo.append((name, l, d2[r0:r0 + rows_per, :], s2[r0:r0 + rows_per, :]))

        def cast_some(k, extra=()):
            for _ in range(k):
                if cast_todo:
                    cast_piece(*cast_todo.pop(0), extra=extra)

        cast_w("wmem", 0, wmem_bf[0], w_mem[0], 1024)
        cast_w("win", 0, win_bf[0], w_in[0], 1024)
        rW[("wpool", 0)] = [Res("wpool")]
        S.dma("pool", wpool_bf.rearrange("g c d -> (g c) d"), w_pool.rearrange("g c d -> (g c) d"),
              writes=rW[("wpool", 0)])
        cast_w("wout", 0, wout_bf[0], w_out[0], 1024)
        cast_w("wup", 0, wup_bf[0], w_up[0], 1024, defer=True, rows_per=4096)
        cast_w("wdown", 0, wdown_bf[0], w_down[0], 1024, defer=True, rows_per=4096)
        cast_w("wkvf", 0, wkvf_bf, w_kvf, 1032, defer=True, rows_per=1024)
        cast_w("wmem", 1, wmem_bf[1], w_mem[1], 1024, defer=True)
        cast_w("win", 1, win_bf[1], w_in[1], 1024, defer=True)
        cast_w("wout", 1, wout_bf[1], w_out[1], 1024, defer=True)
        cast_w("wup", 1, wup_bf[1], w_up[1], 1024, defer=True)
        cast_w("wdown", 1, wdown_bf[1], w_down[1], 1024, defer=True)
        r_kbfs = Res("kbfs")
        r_vbfs = Res("vbfs")

        LN_DEPS = float(np.log(D * EPS))

        def sumsq_rstd(src_tile, nch, T, src_res, want_b=False):
            hh = nch // 2
            for a in range(2):
                S.op("act", lambda e, a=a: e.activation(out=sqatt[:, a * hh:(a + 1) * hh, 0:T],
                                                        in_=src_tile[:, a * hh:(a + 1) * hh, 0:T], func=AF.Square),
                     src_res[a * hh:(a + 1) * hh], r_sqc[a * hh:(a + 1) * hh])
            ps, rp = psring.next()
            mm(ps[:, 0:T], [(ones_bf[:], sqatt[:, c, 0:T]) for c in range(nch)], r_sqc[0:nch] + [r_const], [rp])
            act(AF.Ln, lnv[:, 0:T], ps[:, 0:T], [rp], [r_ln], bias=float(D * EPS))
            act(AF.Exp, rstd[:, 0:T], lnv[:, 0:T], [r_ln], [r_rstd], scale=-0.5)
            if want_b:
                act(AF.Exp, rstd2[:, 0:T], lnv[:, 0:T], [r_ln], [r_rstd2], scale=2.0, bias=LN_DEPS)

        def prep_xg(src_tile, src_res, gcol, T, want_b=False):
            for c in (0, 1, 3, 4, 6, 7, 2, 5):
                S.op("act", lambda e, c=c: e.activation(out=hbuf[:, c, 0:T], in_=src_tile[:, c, 0:T], func=AF.Copy,
                                                        scale=g32[:, gcol + c:gcol + c + 1]),
                     [src_res[c], r_g32], [r_h[c]])
            sumsq_rstd(src_tile, KC, T, src_res, want_b=want_b)

        def evac_post(oc, ps_ap, rp, gcol, T):
            S.op("act", lambda e: e.activation(out=zbuf[:, oc, 0:T], in_=ps_ap, func=AF.Copy,
                                               scale=g32[:, gcol + oc:gcol + oc + 1]), [rp, r_g32], [r_z[oc]])
            S.op("act", lambda e: e.activation(out=hbuf[:, oc, 0:T], in_=ps_ap, func=AF.Square), [rp], [r_h[oc]])

        def post_norm_residual(T, bias_tile=None, bias_res=None):
            ps, rp = psring.next()
            mm(ps[:, 0:T], [(ones_bf[:], hbuf[:, c, 0:T]) for c in range(KC)], r_h + [r_const], [rp])
            if bias_tile is None:
                act(AF.Ln, lnv[:, 0:T], ps[:, 0:T], [rp], [r_ln], bias=float(D * EPS))
            else:
                dve_tt(lnv[:, 0:T], ps[:, 0:T], bias_tile[:, 0:T], ALU.add, [rp, bias_res], [r_ln])
                act(AF.Ln, lnv[:, 0:T], lnv[:, 0:T], [r_ln], [r_ln])
            act(AF.Exp, rstd[:, 0:T], lnv[:, 0:T], [r_ln], [r_rstd], scale=-0.5)
            xb, rx = xbufs[X.i], r_xs[X.i]
            for c in range(KC):
                dve_tt(zbuf[:, c, 0:T], zbuf[:, c, 0:T], rstd[:, 0:T], ALU.mult, [r_z[c], r_rstd], [r_z[c]])
                dve_tt(xb[:, c, 0:T], xb[:, c, 0:T], zbuf[:, c, 0:T], ALU.add, [rx[c], r_z[c]], [rx[c]],
                       eng=("pool" if c % 3 == 2 else "dve"))

        def rms_prep(src_tile, src_res, gcol, T):
            sumsq_rstd(src_tile, KC, T, src_res)
            for c in range(KC):
                dve_stt(hbuf[:, c, 0:T], src_tile[:, c, 0:T], g32[:, gcol + c:gcol + c + 1], rstd[:, 0:T],
                        ALU.mult, ALU.mult, [src_res[c], r_rstd, r_g32], [r_h[c]])

        def mem_project(l, sq="pool"):
            T = 256
            S.dma("sp", xbufs[X.i][:, :, 0:T], memT.rearrange("(c p) t -> p c t", p=128), writes=r_xs[X.i])
            rms_prep(xbufs[X.i], r_xs[X.i], G_MEM + 8 * l, T)
            wg, rw = load_w(wmem_bf[l], 0, 8, 0, 512, rW[("wmem", l)])
            for hm in range(4):
                ps, rp = psring.next()
                mm(ps[:, 0:T], [(wg[:, kc, hm * 128:(hm + 1) * 128], hbuf[:, kc, 0:T]) for kc in range(KC)],
                   r_h + [rw], [rp])
                act_copy(zbuf[:, hm, 0:T], ps[:, 0:T], [rp], [r_z[hm]])
                dve_copy(memk[0][:, hm, :], zbuf[:, hm, 0:T], [r_z[hm]], [r_memk[0]])
            S.dma(sq, mkT_p[l].rearrange("(h p) t -> p h t", p=128), zbuf[:, 0:4, 0:T], reads=r_z[0:4])
            wg, rw = load_w(wmem_bf[l], 0, 8, 512, 512, rW[("wmem", l)])
            for kb in range(2):
                ps, rp = psring.next()
                mm(ps[:, :], [(hbuf[:, kc, kb * 128:(kb + 1) * 128], wg[:, kc, :]) for kc in range(KC)],
                   r_h + [rw], [rp])
                act_copy(zbuf[:, 4 + kb, :], ps[:, :], [rp], [r_z[4 + kb]])
                dve_copy(memv[0][:, kb, :], zbuf[:, 4 + kb, :], [r_z[4 + kb]], [r_memv[0]])
            S.dma(sq, mv_p[l].rearrange("(b p) n -> p b n", p=128), zbuf[:, 4:6, :], reads=r_z[4:6])

        def mem_load_sample(l):
            S.dma("sp", zbuf[:, 0:4, 0:256], cmkT[l].rearrange("(h p) t -> p h t", p=128), writes=r_z[0:4])
            dve_copy(memk[1][:, :, :], zbuf[:, 0:4, 0:256], r_z[0:4], [r_memk[1]])
            S.dma("sp", zbuf[:, 4:6, :], cmv[l].rearrange("(b p) n -> p b n", p=128), writes=r_z[4:6])
            dve_copy(memv[1][:, :, :], zbuf[:, 4:6, :], r_z[4:6], [r_memv[1]])

        def mem_attention(kind, T):
            MK, MV = memk[kind], memv[kind]

            def issue_S(hm):
                out = []
                for kb in range(2):
                    s_ps, r_s = psring.next()
                    mm(s_ps[:, 0:T], [(MK[:, hm, kb * 128:(kb + 1) * 128], qm[:, hm, 0:T])],
                       [r_memk[kind], r_qm[hm]], [r_s])
                    out.append((s_ps, r_s))
                return out
            nxt = issue_S(0)
            for hm in range(4):
                cur = nxt
                pts = []
                for kb in range(2):
                    s_ps, r_s = cur[kb]
                    pt, r_pt = ptmp.next()
                    act(AF.Exp, pt[:, 0:T], s_ps[:, 0:T], [r_s], [r_pt], scale=float(128 ** -0.5))
                    pts.append((pt, r_pt))
                if hm + 1 < 4:
                    nxt = issue_S(hm + 1)
                o_ps, r_o = accring.next()
                d_ps, r_d = accring.next()
                for kb in range(2):
                    pt, r_pt = pts[kb]
                    mm(o_ps[:, 0:T], [(MV[:, kb, hm * 128:(hm + 1) * 128], pt[:, 0:T])], [r_memv[kind], r_pt], [r_o],
                       first=(kb == 0), last=(kb == 1))
                    mm(d_ps[:, 0:T], [(ones_bf[:], pt[:, 0:T])], [r_const, r_pt], [r_d],
                       first=(kb == 0), last=(kb == 1))
                act(AF.Ln, lnv[:, 0:T], d_ps[:, 0:T], [r_d], [r_ln])
                act(AF.Exp, rden[:, 0:T], lnv[:, 0:T], [r_ln], [r_rden], scale=-1.0)
                dve_tt(sqatt[:, 4 + hm, 0:T], o_ps[:, 0:T], rden[:, 0:T], ALU.mult, [r_o, r_rden], [r_att[4 + hm]])

        def mlp(l, T):
            prep_xg(xbufs[X.i], r_xs[X.i], G_MLP_PRE + 8 * l, T, want_b=True)
            for og in range(8):
                if og == 4:
                    cast_some(1)
                wg, rw = load_w(wup_bf[l], 0, 8, og * 512, 512, rW[("wup", l)])
                for j in range(4):
                    ps, rp = psring.next()
                    mm(ps[:, 0:T], [(wg[:, kc, j * 128:(j + 1) * 128], hbuf[:, kc, 0:T]) for kc in range(KC)],
                       r_h + [rw], [rp])
                    rt, r_rt = rtmp.next()
                    act(AF.Relu, rt[:, 0:T], ps[:, 0:T], [rp], [r_rt])
                    dve_tt(a_v[:, og * 4 + j, 0:T], rt[:, 0:T], rt[:, 0:T], ALU.mult, [r_rt], [r_a[og]])
            for half in range(2):
                if half == 1:
                    cast_some(1)
                accs = [(accring if half == 0 else psring).next() for _ in range(4)]
                for kg in range(4):
                    wg, rw = load_w(wdown_bf[l], kg * 8, 8, half * 512, 512, rW[("wdown", l)])

                    def emit(e, wg=wg, kg=kg, accs=accs):
                        for j in range(4):
                            for kc in range(8):
                                ins = e.matmul(accs[j][0][:, 0:T], lhsT=wg[:, kc, j * 128:(j + 1) * 128],
                                               rhs=a_v[:, kg * 8 + kc, 0:T],
                                               start=(kg == 0 and kc == 0), stop=(kg == 3 and kc == 7))
                        return ins
                    if kg < 3:
                        S.op("pe", emit, r_a[2 * kg:2 * kg + 2] + [rw], [a[1] for a in accs])
                    else:
                        for j in range(4):
                            def emit_j(e, wg=wg, kg=kg, accs=accs, j=j):
                                for kc in range(8):
                                    ins = e.matmul(accs[j][0][:, 0:T], lhsT=wg[:, kc, j * 128:(j + 1) * 128],
                                                   rhs=a_v[:, kg * 8 + kc, 0:T], start=False, stop=(kc == 7))
                                return ins
                            S.op("pe", emit_j, r_a[2 * kg:2 * kg + 2] + [rw], [accs[j][1]])
                            evac_post(half * 4 + j, accs[j][0][:, 0:T], accs[j][1], G_MLP_POST + 8 * l, T)
            post_norm_residual(T, bias_tile=rstd2, bias_res=r_rstd2)

        def kvf(kind, t, T):
            rms_prep(xbufs[X.i], r_xs[X.i], G_KV, T)
            wf, rwf = wf_sb, r_wf
            S.dma("sp", wf_sb[:], wkvf_bf[:, 1024:1032].rearrange("(c p) n -> p c n", p=128), reads=rW[("wkvf", 0)],
                  writes=[r_wf])
            nb = max(1, T // 128)
            tn = min(128, T)
            W8 = nb * 8
            psf, rpf = psring.next()
            for b in range(nb):
                tok = slice(b * 128, b * 128 + tn)
                mm(psf[0:tn, b * 8:(b + 1) * 8], [(hbuf[:, kc, tok], wf[:, kc, 0:8]) for kc in range(KC)],
                   r_h + [rwf], [rpf])
            dve_tt(fl[0:tn, 0:W8], psf[0:tn, 0:W8], bfb_sb[0:tn, 0:W8], ALU.add, [rpf, r_const], [r_fl])
            act(AF.Exp, fe[0:tn, 0:W8], fl[0:tn, 0:W8], [r_fl], [r_fe], scale=-1.0)
            act(AF.Ln, fl[0:tn, 0:W8], fe[0:tn, 0:W8], [r_fe], [r_fl], bias=1.0)
            lf, r_lf = lfr.next()
            dve_ts(lf[0:tn, 0:W8], fl[0:tn, 0:W8], -1.0, None, ALU.mult, ALU.bypass, [r_fl], [r_lf])
            if kind == 0:
                blk0 = t * 4
                S.dma("pool", logf_p[t * TT:(t + 1) * TT, :].rearrange("(b p) h -> p b h", p=128),
                      lf[:, 0:W8].rearrange("p (b h) -> p b h", h=8), reads=[r_lf])
                prevC = Call[:, blk0 - 1, :] if blk0 > 0 else zeros8[:]
                r_prev = r_Call[blk0 - 1] if blk0 > 0 else r_zeros8
            else:
                S.dma("pool", logf_s[0:tn, :], lf[0:tn, 0:8], reads=[r_lf])
                prevC, r_prev = Call_s[:, NSB - 1, :], r_Calls[NSB - 1]
            wg, rw = load_w(wkvf_bf, 0, 8, 0, 512, rW[("wkvf", 0)])
            for hd in range(8):
                ps, rp = psring.next()
                mm(ps[0:64, 0:T], [(wg[:, kc, hd * 64:(hd + 1) * 64], hbuf[:, kc, 0:T]) for kc in range(KC)],
                   r_h + [rw], [rp])
                act_copy(zbuf[0:64, hd, 0:T], ps[0:64, 0:T], [rp], [r_z[hd]])
            if kind == 0:
                c0 = t * TT
                S.dma("pool", kT_p[:, c0:c0 + T].rearrange("(h d) t -> d h t", d=64), zbuf[0:64, :, 0:T], reads=r_z)
                S.dma("pool", kbf[:, c0:c0 + T].rearrange("(h d) t -> d h t", d=64), zbuf[0:64, :, 0:T], reads=r_z,
                      writes=[r_kbf])
            else:
                S.dma("pool", kT_s.rearrange("(h d) t -> d h t", d=64), zbuf[0:64, :, 0:T], reads=r_z)
                S.dma("pool", kbf_s[:, PAST:PAST + T].rearrange("(h d) t -> d h t", d=64), zbuf[0:64, :, 0:T],
                      reads=r_z, writes=[r_kbfs])
            wv, rwv = load_w(wkvf_bf, 0, 8, 512, 512, rW[("wkvf", 0)])
            for b in range(nb):
                tn = min(128, T)
                tok = slice(b * 128, b * 128 + tn)
                ps, rp = psring.next()
                mm(ps[0:tn, :], [(hbuf[:, kc, tok], wv[:, kc, :]) for kc in range(KC)],
                   r_h + [rwv], [rp])
                vft, r_vf = vf.next()
                act_copy(vft[0:tn, :], ps[0:tn, :], [rp], [r_vf])
                vs, r_vs = vstr.next()
                dve_copy(vs[0:tn, :, 0:64], vft[0:tn, :].rearrange("p (h e) -> p h e", e=64), [r_vf], [r_vs])
                if kind == 0:
                    row0 = t * TT + b * 128
                    S.dma("pool", v_p[row0:row0 + tn, :], vft[0:tn, :], reads=[r_vf])
                    S.dma("pool", vbf[row0:row0 + tn, :], vs[0:tn, :, :].rearrange("p h e -> p (h e)"), reads=[r_vs],
                          writes=[r_vbf])
                else:
                    S.dma("pool", v_s[0:tn, :], vft[0:tn, :], reads=[r_vf])
                    S.dma("pool", vbf_s[PAST:PAST + tn, :], vs[0:tn, :, :].rearrange("p h e -> p (h e)"),
                          reads=[r_vs], writes=[r_vbfs])
            psc, rpc = psring.next()
            for b in range(nb):
                def emit_c(e, b=b):
                    o = psc[0:tn, b * 8:(b + 1) * 8]
                    e.matmul(o, lhsT=tri_sb[0:tn, 0:tn], rhs=lf[0:tn, b * 8:(b + 1) * 8], start=True, stop=False)
                    for b2 in range(b):
                        e.matmul(o, lhsT=onesf[:, 0:tn], rhs=lf[:, b2 * 8:(b2 + 1) * 8], start=False, stop=False)
                    return e.matmul(o, lhsT=lastrow_sb[:, 0:tn], rhs=prevC, start=False, stop=True)
                S.op("pe", emit_c, [r_lf, r_prev, r_const], [rpc])

                def emit_b(e, b=b):
                    o = psc[:, 32 + b * 8:32 + (b + 1) * 8]
                    for b2 in range(b + 1):
                        e.matmul(o, lhsT=onesf[0:tn, :], rhs=lf[0:tn, b2 * 8:(b2 + 1) * 8], start=(b2 == 0), stop=False)
                    return e.matmul(o, lhsT=lastrow_sb[:, :], rhs=prevC, start=False, stop=True)
                S.op("pe", emit_b, [r_lf, r_prev, r_const], [rpc])
            if kind == 0:
                dve_copy(Call[:, blk0:blk0 + nb, :], psc[:, 0:W8].rearrange("p (b h) -> p b h", h=8), [rpc],
                         r_Call[blk0:blk0 + nb])
                dve_copy(Cb_all[:, blk0:blk0 + nb, :], psc[:, 32:32 + W8].rearrange("p (b h) -> p b h", h=8), [rpc],
                         r_Cb[blk0:blk0 + nb])
                for j in (2 * t, 2 * t + 1):
                    dve_ts(Cb_own[:, j, :], Cb_all[:, 2 * j, :], rflag_sb[:, 1:2], None, ALU.mult, ALU.bypass,
                           [r_Cb[2 * j], r_const], [r_Cbo[j]])
                    dve_stt(Cb_own[:, j, :], Cb_all[:, 2 * j + 1, :], rflag_sb[:, 0:1], Cb_own[:, j, :], ALU.mult, ALU.add,
                            [r_Cb[2 * j + 1], r_Cbo[j], r_const], [r_Cbo[j]])
            else:
                dve_copy(Call_s[0:tn, NSB, :], psc[0:tn, 0:8], [rpc], [r_Calls[NSB]])
                dve_copy(Cb_s[:], psc[:, 32:40], [rpc], [r_Cbs])

        def cum_block(lf, r_lf, tn, prevC, r_prev, dstC, r_dst):
            ps, rp = psring.next()

            def emit(e):
                e.matmul(ps[0:tn, 0:8], lhsT=tri_sb[0:tn, 0:tn], rhs=lf[0:tn, :], start=True, stop=False)
                return e.matmul(ps[0:tn, 0:8], lhsT=lastrow_sb[:, 0:tn], rhs=prevC, start=False, stop=True)
            S.op("pe", emit, [r_lf, r_prev, r_const], [rp])
            dve_copy(dstC[0:tn] if tn < 128 else dstC, ps[0:tn, 0:8], [rp], [r_dst])

        def bcast_last(srcC, r_src, tn, dst, r_dst):
            ps, rp = psring.next()
            lr = lastrow_sb if tn == 128 else lastrow16_sb
            mm(ps[:, 0:8], [(lr[0:tn, :], srcC[0:tn] if tn < 128 else srcC)], [r_src, r_const], [rp])
            dve_copy(dst, ps[:, 0:8], [rp], [r_dst])

        r_kbf = Res("kbf")
        r_vbf = Res("vbf")
        r_x1own = Res("x1own")
        r_x1s = Res("x1s")

        pre = {"key": None, "buf": 0}

        def issue_x_load(key, bufi):
            layer, kind, t = key
            T = TT if kind == 0 else ST
            if layer == 0:
                src, rd = (xT[:, t * TT:t * TT + T], []) if kind == 0 else (xsT, [])
            else:
                src, rd = (x1own[:, t * TT:t * TT + T], [r_x1own]) if kind == 0 else (x1s, [r_x1s])
            S.dma("sp", xbufs[bufi][:, :, 0:T], src.rearrange("(c p) t -> p c t", p=128), reads=rd, writes=r_xs[bufi])

        def begin_tile(key):
            if pre["key"] == key:
                X.i = pre["buf"]
            else:
                issue_x_load(key, X.i)
            pre["key"] = None

        def prefetch_x(key):
            if key is None:
                return
            b = 1 - X.i
            issue_x_load(key, b)
            pre["key"], pre["buf"] = key, b

        def layer0_tile(kind, t, T, nxt=None):
            W = 15 + T
            begin_tile((0, kind, t))
            xi = X.i
            if kind == 0:
                if t == 0:
                    memset(u_v[:, :, 0:15], 0.0, [r_u])
            else:
                S.dma("sp", u_v[:, :, 0:15], poolprev.rearrange("(c p) t -> p c t", p=128), writes=[r_u])
            prep_xg(xbufs[X.i], r_xs[X.i], G_MIX_PRE, T)
            wg, rw = load_w(win_bf[0], 0, 8, 0, 512, rW[("win", 0)])
            for oc in range(4):
                ps, rp = psring.next()
                mm(ps[:, 0:T], [(wg[:, kc, oc * 128:(oc + 1) * 128], hbuf[:, kc, 0:T]) for kc in range(KC)],
                   r_h + [rw], [rp])
                dve_tt(u_v[:, oc, 15:W], ps[:, 0:T], rstd[:, 0:T], ALU.mult, [rp, r_rstd], [r_u])
            wg, rw = load_w(win_bf[0], 0, 8, 512, 512, rW[("win", 0)])
            for hm in range(4):
                ps, rp = psring.next()
                mm(ps[:, 0:T], [(wg[:, kc, hm * 128:(hm + 1) * 128], hbuf[:, kc, 0:T]) for kc in range(KC)],
                   r_h + [rw], [rp])
                dve_tt(qm[:, hm, 0:T], ps[:, 0:T], rstd[:, 0:T], ALU.mult, [rp, r_rstd], [r_qm[hm]])
            if kind == 0 and t == 0:
                cast_some(3, extra=[rw] + r_xs[xi])
            else:
                cast_some(1)
            if kind == 0 and t == 0:
                S.dma("sp", wpool_sb[:], wpool_bf.rearrange("g c d -> c g d"), reads=rW[("wpool", 0)],
                      writes=[r_wpool])
            yield "head"
            X.i = xi
            prefetch_x(nxt)
            if kind == 0 and t == NT0 - 2:
                sample_cache_prep()
            for g in range(4):
                w = 2 << g
                ug = u_v[:, g, :]
                dve_tt(sA[:, 1:W], ug[:, 1:W], ug[:, 0:W - 1], ALU.add, [r_u], [r_sA])
                sw, r_sw = sA, r_sA
                if g >= 1:
                    dve_tt(sB[:, 3:W], sA[:, 3:W], sA[:, 1:W - 2], ALU.add, [r_sA], [r_sB])
                    sw, r_sw = sB, r_sB
                if g >= 2:
                    dve_tt(sA[:, 7:W], sB[:, 7:W], sB[:, 3:W - 4], ALU.add, [r_sB], [r_sA])
                    sw, r_sw = sA, r_sA
                if g >= 3:
                    dve_tt(sB[:, 15:W], sA[:, 15:W], sA[:, 7:W - 8], ALU.add, [r_sA], [r_sB])
                    sw, r_sw = sB, r_sB
                if kind == 0 and t == 0:
                    dve_tt(sw[:, 15:31], sw[:, 15:31], cntfix_sb[:, g, :], ALU.mult, [r_sw, r_const], [r_sw])
                dve_stt(pooled[:, g, 0:T], sw[:, 15:W], 1.0 / w, ug[:, 15:W], ALU.mult, ALU.subtract,
                        [r_sw, r_u], [r_pooled[g]])
            mem_attention(kind, T)
            for g in range(4):
                ps, rp = psring.next()
                mm(ps[:, 0:T], [(wpool_sb[:, g, :], pooled[:, g, 0:T])], [r_wpool, r_pooled[g]], [rp])
                dve_stt(sqatt[:, g, 0:T], ps[:, 0:T], pscale_sb[:, g:g + 1], onesT[:, 0:T], ALU.mult, ALU.mult,
                        [rp, r_const], [r_att[g]])
            dve_copy(u_v[:, :, 0:15], u_v[:, :, T:T + 15], [r_u], [r_u])
            if kind == 0 and t == NT0 - 1:
                S.dma("pool", pstate_p.rearrange("(c p) t -> p c t", p=128), u_v[:, :, 0:15], reads=[r_u])
            if kind == 1:
                S.dma("pool", pstate_s.rearrange("(c p) t -> p c t", p=128), u_v[:, :, 0:15], reads=[r_u])
            wgs = [load_w(wout_bf[0], 0, 8, 0, 512, rW[("wout", 0)]), load_w(wout_bf[0], 0, 8, 512, 512, rW[("wout", 0)])]
            cast_some(1)
            for oc in range(8):
                wg, rw = wgs[oc // 4]
                j = oc % 4
                ps, rp = psring.next()
                mm(ps[:, 0:T], [(wg[:, kc, j * 128:(j + 1) * 128], sqatt[:, kc, 0:T]) for kc in range(KC)],
                   r_sqc + [rw], [rp])
                evac_post(oc, ps[:, 0:T], rp, G_MIX_POST, T)
            post_norm_residual(T)
            mlp(0, T)
            yield "mid"
            X.i = xi
            kvf(kind, t, T)
            if kind == 0:
                for c in range(KC):
                    xv = xbufs[X.i][:, c, :].rearrange("p (a r q) -> p a r q", r=2, q=128)
                    xo = xown[:, c, :].rearrange("p (a q) -> p a q", q=128)
                    dve_ts(xo, xv[:, :, 0, :], rflag_sb[:, 1:2], None, ALU.mult, ALU.bypass,
                           [r_xs[X.i][c], r_const], r_a[0:2])
                    dve_stt(xo, xv[:, :, 1, :], rflag_sb[:, 0:1], xo, ALU.mult, ALU.add,
                            [r_xs[X.i][c], r_const] + r_a[0:2], r_a[0:2])
                S.dma("pool", x1own[:, t * 256:(t + 1) * 256].rearrange("(c p) t -> p c t", p=128), xown,
                      reads=r_a[0:2], writes=[r_x1own])
            else:
                S.dma("pool", x1s.rearrange("(c p) t -> p c t", p=128), xbufs[X.i][:, :, 0:T], reads=r_xs[X.i], writes=[r_x1s])

        def layer1_tile(kind, t, T, nxt=None):
            begin_tile((1, kind, t))
            if kind == 0:
                nkb = 8 * t + 8
                kr = 66
                kcols = nkb * 128
            else:
                nkb = NSB + 1
                kr = 64
                kcols = PAST + ST
            prep_xg(xbufs[X.i], r_xs[X.i], G_MIX_PRE + 8, T)
            wg, rw = load_w(win_bf[1], 0, 8, 0, 512, rW[("win", 1)])
            for hd in range(8):
                ps, rp = psring.next()
                mm(ps[0:64, 0:T], [(wg[:, kc, hd * 64:(hd + 1) * 64], hbuf[:, kc, 0:T]) for kc in range(KC)],
                   r_h + [rw], [rp])
                dve_stt(qaug[0:64, hd, 0:T], ps[0:64, 0:T], 0.125, rstd[0:64, 0:T], ALU.mult, ALU.mult,
                        [rp, r_rstd], [r_qaug[hd]])
            wg, rw = load_w(win_bf[1], 0, 8, 512, 512, rW[("win", 1)])
            for hm in range(4):
                ps, rp = psring.next()
                mm(ps[:, 0:T], [(wg[:, kc, hm * 128:(hm + 1) * 128], hbuf[:, kc, 0:T]) for kc in range(KC)],
                   r_h + [rw], [rp])
                dve_tt(qm[:, hm, 0:T], ps[:, 0:T], rstd[:, 0:T], ALU.mult, [rp, r_rstd], [r_qm[hm]])
            prefetch_x(nxt)
            mem_attention(kind, T)
            if kind == 0:
                jl = 4 * t + 3
                pstep = Cb_own[:].ap[0][0]
                cref_b = bass.AP(Cb_own[:].tensor, jl * 8, [[pstep, 128], [0, nkb], [1, 8]])
                dve_tt(biasK[:, 0:nkb, :], cref_b, Call[:, 0:nkb, :], ALU.subtract, [r_Cbo[jl]] + r_Call[0:nkb], [r_biasK])
                for m in range(4):
                    dve_tt(offt[:, m, :], Cb_own[:, 4 * t + m, :], Cb_own[:, jl, :], ALU.subtract,
                           [r_Cbo[4 * t + m], r_Cbo[jl]], [r_offt])
                dve_copy(offhi[:], offt[:], [r_offt], [r_offhi])
                dve_tt(offmid[:], offt[:], offhi[:], ALU.subtract, [r_offt, r_offhi], [r_offmid])
                dve_ts(tmp2[:], offmid[:], pmask_sb[:, 1:2], None, ALU.mult, ALU.bypass, [r_offmid, r_const], [r_tmp2])
                dve_stt(src2[:], offhi[:], pmask_sb[:, 0:1], tmp2[:], ALU.mult, ALU.add, [r_offhi, r_tmp2, r_const],
                        [r_src2])
                for hd in range(8):
                    for m in range(4):
                        dve_stt(qaug[64:66, hd, m * 128:(m + 1) * 128], zrow[64:66, :], src2[64:66, m, hd:hd + 1],
                                zrow[64:66, :], ALU.add, ALU.add, [r_src2, r_zrow], [r_qaug[hd]])
                S.dma("sp", vaug_v[:, 0:nkb, :, :].rearrange("p b h e -> p b (h e)"),
                      vbf[0:nkb * 128, :].rearrange("(b p) e -> p b e", p=128), reads=[r_vbf], writes=r_a)
                ksrc, r_ks = kbf, r_kbf
            else:
                pstep = Cb_s[:].ap[0][0]
                cref_b = bass.AP(Cb_s[:].tensor, 0, [[pstep, 128], [0, nkb], [1, 8]])
                dve_tt(biasK[:, 0:nkb, :], cref_b, Call_s[:, 0:nkb, :], ALU.subtract, [r_Cbs] + r_Calls, [r_biasK])
                S.dma("sp", vaug_v[:, 0:NSB, :, :].rearrange("p b h e -> p b (h e)"),
                      vbf_s[0:PAST, :].rearrange("(b p) e -> p b e", p=128), reads=[r_vbfs], writes=r_a)
                S.dma("sp", vaug_v[0:ST, NSB, :, :].rearrange("p h e -> p (h e)"), vbf_s[PAST:PAST + ST, :],
                      reads=[r_vbfs], writes=r_a)
                ksrc, r_ks = kbf_s, r_kbfs
            pending = [None]

            def finish_head(hd, acc, r_acc):
                act(AF.Ln, lnv[64:65, 0:T], acc[64:65, 0:T], [r_acc], [r_ln])
                act(AF.Exp, rden[64:65, 0:T], lnv[64:65, 0:T], [r_ln], [r_rden], scale=-1.0)
                bc, r_bc = psring.next()
                mm(bc[0:64, 0:T], [(onesf[64:65, 0:64], rden[64:65, 0:T])], [r_const, r_rden], [r_bc])
                act_copy(rstd[0:64, 0:T], bc[0:64, 0:T], [r_bc], [r_rstd])
                dve_tt(attnh[0:64, hd, 0:T], acc[0:64, 0:T], rstd[0:64, 0:T], ALU.mult, [r_acc, r_rstd], [r_attnh[hd]])

            for hd in range(8):
                kh, r_kh = khring.next()
                S.dma("sp", kh[0:64, 0:kcols], ksrc[hd * 64:(hd + 1) * 64, 0:kcols], reads=[r_ks], writes=[r_kh])
                acc, r_acc = accring.next()

                def issue_S(i, hd=hd, kh=kh, r_kh=r_kh):
                    kn = 128 if (kind == 0 or i < NSB) else ST
                    c0 = ((i - 8 * t) // 2) * 128 if (kind == 0 and i >= 8 * t) else 0
                    s_ps, r_s = psring.next()
                    pairs = [(kh[0:kr, i * 128:i * 128 + kn], qaug[0:kr, hd, c0:T])]
                    rd = [r_kh, r_qaug[hd], r_khones]
                    if kind == 0 and i >= 8 * t:
                        pairs.append((ident[:, :], masks_sb[:, i - 8 * t, c0:T]))
                        rd.append(r_const)
                    if kind == 1 and i == NSB:
                        pairs.append((ident[0:kn, 0:kn], mask_s_sb[0:kn, 0:T]))
                        rd.append(r_const)
                    mm(s_ps[0:kn, c0:T], pairs, rd, [r_s])
                    return s_ps, r_s, kn, c0
                DEPTH = 3
                q_s = [issue_S(i) for i in range(min(DEPTH, nkb))]
                if pending[0] is not None:
                    finish_head(*pending[0])
                    pending[0] = None
                for i in range(nkb):
                    s_ps, r_s, kn, c0 = q_s.pop(0)
                    pt, r_pt = ptmp.next()
                    act(AF.Exp, pt[0:kn, c0:T], s_ps[0:kn, c0:T], [r_s, r_biasK], [r_pt],
                        bias=biasK[0:kn, i, hd:hd + 1])
                    if i + DEPTH < nkb:
                        q_s.append(issue_S(i + DEPTH))
                    mm(acc[0:65, c0:T], [(vaug_v[0:kn, i, hd, :], pt[0:kn, c0:T])], r_a + [r_pt], [r_acc],
                       first=(i == 0), last=(i == nkb - 1))
                pending[0] = (hd, acc, r_acc)
            finish_head(*pending[0])
            for half in range(2):
                wmx, r_wmx = wring.next()
                S.dma("sp", wmx[0:64, :, :], wout_bf[1][0:512, half * 512:(half + 1) * 512].rearrange(
                    "(h d) n -> d h n", d=64), reads=rW[("wout", 1)], writes=[r_wmx])
                wme, r_wme = load_w(wout_bf[1], 4, 4, half * 512, 512, rW[("wout", 1)])
                for j in range(4):
                    oc = half * 4 + j
                    ps, rp = psring.next()
                    pairs = [(wmx[0:64, hd, j * 128:(j + 1) * 128], attnh[0:64, hd, 0:T]) for hd in range(8)]
                    pairs += [(wme[:, hm, j * 128:(j + 1) * 128], sqatt[:, 4 + hm, 0:T]) for hm in range(4)]
                    mm(ps[:, 0:T], pairs, r_attnh + r_sqc[4:8] + [r_wmx, r_wme], [rp])
                    evac_post(oc, ps[:, 0:T], rp, G_MIX_POST + 8, T)
            post_norm_residual(T)
            mlp(1, T)
            if kind == 0:
                S.dma("pool", yT_own[:, t * TT:t * TT + T].rearrange("(c p) t -> p c t", p=128), xbufs[X.i][:, :, 0:T],
                      reads=r_xs[X.i])
            else:
                S.dma("pool", ysT.rearrange("(c p) t -> p c t", p=128), xbufs[X.i][:, :, 0:T], reads=r_xs[X.i])

        r_khones = Res("khones")

        if "m" not in SK:
            mem_project(0, sq="sp")
        def sample_cache_prep():
            mem_load_sample(0)
            S.dma("sp", clf_sb[:], clf.rearrange("(b p) h -> p b h", p=128), writes=[r_clf])
            ps, rp = psring.next()
            for b in range(NSB):
                def emit(e, ps=ps, b=b):
                    o = ps[:, b * 8:(b + 1) * 8]
                    e.matmul(o, lhsT=tri_sb[:, :], rhs=clf_sb[:, b, :], start=True, stop=(b == 0))
                    ins = None
                    for b2 in range(b):
                        ins = e.matmul(o, lhsT=onesf[:, :], rhs=clf_sb[:, b2, :], start=False, stop=(b2 == b - 1))
                    return ins if ins is not None else e.matmul(o, lhsT=tri_sb[:, :], rhs=clf_sb[:, b, :],
                                                                start=True, stop=True)
                S.op("pe", emit, [r_clf, r_const], [rp])
            dve_copy(Call_s[:, 0:NSB, :], ps[:, 0:NSB * 8].rearrange("p (b h) -> p b h", h=8), [rp], r_Calls[0:NSB])
            for b in range(NSB if "v" not in SK else 0):
                vft, r_vf = vf.next()
                S.dma("sp", vft[:], cv[b * 128:(b + 1) * 128, :], writes=[r_vf])
                vs, r_vs = vstr.next()
                dve_copy(vs[:, :, 0:64], vft[:].rearrange("p (h e) -> p h e", e=64), [r_vf], [r_vs])
                S.dma("pool", vbf_s[b * 128:(b + 1) * 128, :], vs[:, :, :].rearrange("p h e -> p (h e)"), reads=[r_vs],
                      writes=[r_vbfs])


        if stage >= 1:
            tiles = [layer0_tile(0, t, TT, nxt=((0, 0, t + 1) if t + 1 < NT0 else (0, 1, 0))) for t in range(NT0)]
            tiles.append(layer0_tile(1, 0, ST))

            def step(g):
                try:
                    next(g)
                except StopIteration:
                    pass
            step(tiles[0])
            step(tiles[0])
            for k in range(1, len(tiles)):
                step(tiles[k])
                step(tiles[k - 1])
                step(tiles[k])
            step(tiles[-1])
        if stage >= 2:
            cast_some(len(cast_todo))
            S.dma("pool", kbf_s[:, 0:PAST], ckT, writes=[r_kbfs])
            for kh in khs:
                memset(kh[64:66, :], 1.0, [r_khones, r_u, r_sA, r_sB, khring.items[0][1], khring.items[1][1]]
                       + r_pooled + r_qaug)
            mem_project(1)
            mem_load_sample(1)
            for t in range(NT1):
                layer1_tile(0, t, TT, nxt=((1, 0, t + 1) if t + 1 < NT1 else (1, 1, 0)))
            layer1_tile(1, 0, ST)
        S.finish("pool")
        S.emit_all()
    return nc


_NC_CACHE = {}


def _consts():
    bf = ml_dtypes.bfloat16
    k = np.arange(128)[:, None]
    q = np.arange(128)[None, :]
    diag = np.where(k <= q, 0.0, NEG).astype(np.float32)
    masks = []
    for r in range(2):
        mr = np.zeros((8, 128, TT), np.float32)
        for d in range(8):
            for m in range(4):
                tb = 2 * m + r
                if d < tb:
                    blk = np.zeros((128, 128), np.float32)
                elif d == tb:
                    blk = diag
                else:
                    blk = np.full((128, 128), NEG, np.float32)
                mr[d, :, m * 128:(m + 1) * 128] = blk
        masks.append(mr.astype(bf))
    mask_s = np.full((128, TT), NEG, np.float32)
    mask_s[:, 0:128] = diag
    mask_s = mask_s.astype(bf)
    cntfix = np.ones((128, 4, 16), np.float32)
    for g in range(4):
        w = 2 << g
        for t in range(16):
            cntfix[:, g, t] = w / min(w, t + 1)
    tri = (np.arange(128)[:, None] <= np.arange(128)[None, :]).astype(np.float32)
    lastrow = np.zeros((128, 128), np.float32)
    lastrow[127, :] = 1.0
    lastrow16 = np.zeros((128, 128), np.float32)
    lastrow16[ST - 1, :] = 1.0
    pmask = np.zeros((128, 2), np.float32)
    pmask[64, 0] = 1.0
    pmask[65, 1] = 1.0
    ident = np.eye(128, dtype=np.float32).astype(bf)
    return dict(masks=masks, mask_s=mask_s, cntfix=cntfix, tri=tri, lastrow=lastrow, lastrow16=lastrow16,
                pmask=pmask, ident=ident)


def _col(v):
    v = np.asarray(v, np.float32)
    return np.ascontiguousarray(v.reshape(-1, 128).T)


def kernel(x_prompt, x_sample, cache_pool, cache_k, cache_v, cache_logf, cache_mem_k, cache_mem_v,
           mem_prompt, g_mix_pre, g_mix_post, g_mlp_pre, g_mlp_post, w_in, w_out, w_pool, pool_scale,
           g_kv, w_kvf, b_f, g_mem, w_mem_kv, w_up, w_down, _stage=99):
    f32 = np.float32
    A = lambda a: np.ascontiguousarray(np.asarray(a, f32))
    x_prompt, x_sample = A(x_prompt), A(x_sample)
    cs = _consts()
    gains = np.concatenate([_col(g_mix_pre[0]), _col(g_mix_pre[1]), _col(g_mix_post[0]), _col(g_mix_post[1]),
                            _col(g_mlp_pre[0]), _col(g_mlp_pre[1]), _col(g_mlp_post[0]), _col(g_mlp_post[1]),
                            _col(g_mem[0]), _col(g_mem[1]), _col(g_kv)], axis=1)
    shared = dict(w_in=A(w_in), w_out=A(w_out), w_pool=A(w_pool)[0], w_kvf=A(w_kvf), w_mem=A(w_mem_kv),
                  w_up=A(w_up), w_down=A(w_down), gains=np.ascontiguousarray(gains),
                  pscale=_col(np.asarray(pool_scale)[0]),
                  bfb=np.ascontiguousarray(np.broadcast_to(np.tile(np.asarray(b_f, f32), 4)[None, :], (128, 32))),
                  mask_s=cs["mask_s"], cntfix=cs["cntfix"], tri=cs["tri"], lastrow=cs["lastrow"],
                  lastrow16=cs["lastrow16"], pmask=cs["pmask"], ident=cs["ident"])
    xT_seq = [np.ascontiguousarray(x_prompt[b].T) for b in range(4)]
    memT_seq = [np.ascontiguousarray(np.asarray(mem_prompt, f32)[b].T) for b in range(4)]
    in_maps = []
    for c in range(8):
        b, r = c // 2, c % 2
        m = dict(shared)
        m["xT"] = xT_seq[b]
        m["memT"] = memT_seq[b]
        m["xsT"] = np.ascontiguousarray(x_sample[c].T)
        m["poolprev"] = np.ascontiguousarray(np.asarray(cache_pool, f32)[0, c].T)
        m["ckT"] = np.ascontiguousarray(np.asarray(cache_k, f32)[c].reshape(PAST, 512).T)
        m["cv"] = np.ascontiguousarray(np.asarray(cache_v, f32)[c].reshape(PAST, 512))
        m["clf"] = A(np.asarray(cache_logf)[c])
        m["cmkT"] = np.ascontiguousarray(np.asarray(cache_mem_k, f32)[:, c].reshape(2, 256, 512).transpose(0, 2, 1))
        m["cmv"] = np.ascontiguousarray(np.asarray(cache_mem_v, f32)[:, c].reshape(2, 256, 512))
        m["flag"] = np.full((128, 256), r, np.int32)
        m["rflag"] = np.ascontiguousarray(np.broadcast_to(np.array([[r, 1 - r]], np.float32), (128, 2)))
        m["masks"] = cs["masks"][r]
        in_maps.append(m)
    key = int(_stage)
    if key not in _NC_CACHE:
        _NC_CACHE[key] = build(key)
    nc = _NC_CACHE[key]
    ncores = int(os.environ.get("KCORES", "8"))
    res = run_bass_kernel_spmd(nc, in_maps[:ncores], core_ids=list(range(ncores))).results
    res = [res[c % ncores] for c in range(8)]

    y_prompt = np.empty((4, SEQ, D), f32)
    y_sample = np.empty((8, ST, D), f32)
    ps_p = np.empty((1, 4, 15, 512), f32)
    ps_s = np.empty((1, 8, 15, 512), f32)
    k_p = np.empty((4, SEQ, 8, 64), f32)
    v_p = np.empty((4, SEQ, 8, 64), f32)
    lf_p = np.empty((4, SEQ, 8), f32)
    k_s = np.empty((8, ST, 8, 64), f32)
    v_s = np.empty((8, ST, 8, 64), f32)
    lf_s = np.empty((8, ST, 8), f32)
    mk_p = np.empty((2, 4, 256, 4, 128), f32)
    mv_p = np.empty((2, 4, 256, 4, 128), f32)
    for c in range(8):
        b, r = c // 2, c % 2
        o = res[c]
        yo = o["yT_own"].T.reshape(NBLK // 2, 128, D)
        y_prompt[b].reshape(NBLK // 2, 2, 128, D)[:, r] = yo
        y_sample[c] = o["ysT"].T
        ps_s[0, c] = o["pstate_s"].T
        k_s[c] = o["kT_s"].T.reshape(ST, 8, 64)
        v_s[c] = o["v_s"].reshape(ST, 8, 64)
        lf_s[c] = o["logf_s"]
        if r == 0:
            ps_p[0, b] = o["pstate_p"].T
            k_p[b] = o["kT_p"].T.reshape(SEQ, 8, 64)
            v_p[b] = o["v_p"].reshape(SEQ, 8, 64)
            lf_p[b] = o["logf_p"]
            mk_p[:, b] = o["mkT_p"].transpose(0, 2, 1).reshape(2, 256, 4, 128)
            mv_p[:, b] = o["mv_p"].reshape(2, 256, 4, 128)
    return (y_prompt, y_sample, ps_p, ps_s, k_p, v_p, lf_p, k_s, v_s, lf_s, mk_p, mv_p)
```

```python
import os
from contextlib import ExitStack

import numpy as np
import ml_dtypes
import concourse.bass as bass
import concourse.mybir as mybir
from concourse.bass_utils import run_bass_kernel_spmd

F32 = mybir.dt.float32
BF16 = mybir.dt.bfloat16
I32 = mybir.dt.int32
AF = mybir.ActivationFunctionType
ALU = mybir.AluOpType

D = 1024
KC = 8
SEQ = 4096
TT = 512
NT0 = SEQ // TT
NT1 = NT0 // 2
NBLK = SEQ // 128
PAST = 2048
NSB = PAST // 128
ST = 16
EPS = 1e-6
NEG = -30000.0

G_MIX_PRE, G_MIX_POST, G_MLP_PRE, G_MLP_POST, G_MEM = 0, 16, 32, 48, 64
G_KV = 80
NG = 88


class Res:
    __slots__ = ("name", "last_write", "reads", "extra")

    def __init__(self, name=""):
        self.name = name
        self.last_write = None
        self.reads = {}
        self.extra = []


class Sched:
    COMPUTE = ("pe", "act", "dve", "pool")

    def __init__(self, nc):
        self.nc = nc
        self.streams = {e: [] for e in ("pe", "act", "dve", "pool", "sp")}
        self.sems = {}
        self.cnt = {}
        self.seen = {e: {} for e in self.streams}
        self.ring = {"sp": 24, "act": 4, "pool": 20}
        self.ring_pos = {q: 0 for q in self.ring}

    def open(self, stack):
        nc = self.nc
        for e in self.COMPUTE:
            self.sems[e] = stack.enter_context(nc.semaphore("s_" + e))
            self.cnt[e] = 0
        for q, n in self.ring.items():
            for i in range(n):
                src = ("dma", q, i)
                self.sems[src] = stack.enter_context(nc.semaphore("d_%s%d" % (q, i)))
                self.cnt[src] = 0

    def _deps(self, reads, writes, merge=False):
        deps = {}

        def add(d):
            if d is not None and d[1] > deps.get(d[0], 0):
                deps[d[0]] = d[1]
        for r in reads:
            add(r.last_write)
            for d in r.extra:
                add(d)
        for w in writes:
            if not merge:
                add(w.last_write)
                for d in w.extra:
                    add(d)
            for s, c in w.reads.items():
                add((s, c))
        return deps

    def _waits(self, eng, deps):
        out = []
        seen = self.seen[eng]
        for s, c in deps.items():
            if c > seen.get(s, 0):
                seen[s] = c
                out.append((self.sems[s], c))
        return out

    def _record(self, src, count, reads, writes, merge=False):
        for r in reads:
            if count > r.reads.get(src, 0):
                r.reads[src] = count
        for w in writes:
            if merge:
                w.extra.append((src, count))
            else:
                w.last_write = (src, count)
                w.extra = []
                w.reads = {}

    def op(self, eng, emit, reads=(), writes=()):
        reads = [r for r in reads if r is not None]
        writes = [w for w in writes if w is not None]
        deps = self._deps(reads, writes)
        if eng in deps and eng == "pe":
            del deps[eng]
        waits = self._waits(eng, deps)
        self.cnt[eng] += 1
        c = self.cnt[eng]
        sem = self.sems[eng]

        def run(e, waits=waits, emit=emit, sem=sem):
            for s, v in waits:
                e.wait_ge(s, v)
            emit(e).then_inc(sem, 1)
        self.streams[eng].append(run)
        self._record(eng, c, reads, writes)

    def dma(self, q, out, in_, reads=(), writes=(), merge=False):
        reads = [r for r in reads if r is not None]
        writes = [w for w in writes if w is not None]
        n = self.ring[q]
        slot = self.ring_pos[q] % n
        self.ring_pos[q] += 1
        src = ("dma", q, slot)
        deps = self._deps(reads, writes, merge)
        if self.cnt[src] > deps.get(src, 0):
            deps[src] = self.cnt[src]
        waits = self._waits(q, deps)
        self.cnt[src] += 16
        c = self.cnt[src]
        sem = self.sems[src]

        def run(e, waits=waits, sem=sem, out=out, in_=in_):
            for s, v in waits:
                e.wait_ge(s, v)
            e.dma_start(out=out, in_=in_).then_inc(sem, 16)
        self.streams[q].append(run)
        self._record(src, c, reads, writes, merge)

    def finish(self, eng="pool"):
        waits = [(self.sems[s], c) for s, c in self.cnt.items() if c > 0 and s != eng]

        def run(e, waits=waits):
            for s, v in waits:
                e.wait_ge(s, v)
        self.streams[eng].append(run)

    def emit_all(self):
        with self.nc.Block() as block:
            @block.tensor
            def _(e):
                for f in self.streams["pe"]:
                    f(e)

            @block.scalar
            def _(e):
                for f in self.streams["act"]:
                    f(e)

            @block.vector
            def _(e):
                for f in self.streams["dve"]:
                    f(e)

            @block.gpsimd
            def _(e):
                for f in self.streams["pool"]:
                    f(e)

            @block.sync
            def _(e):
                for f in self.streams["sp"]:
                    f(e)


class Ring:
    def __init__(self, items):
        self.items = items
        self.i = 0

    def next(self):
        it = self.items[self.i % len(self.items)]
        self.i += 1
        return it


def build(stage=99):
    nc = bass.Bass("TRN2", target_bir_lowering=False)

    def din(name, shape, dt=F32):
        return nc.dram_tensor(name, list(shape), dt, kind="ExternalInput").ap()

    def dout(name, shape, dt=F32):
        return nc.dram_tensor(name, list(shape), dt, kind="ExternalOutput").ap()

    def dint(name, shape, dt=BF16):
        return nc.dram_tensor(name, list(shape), dt, kind="Internal").ap()

    xT = din("xT", [D, SEQ])
    memT = din("memT", [D, 256])
    xsT = din("xsT", [D, ST])
    poolprev = din("poolprev", [512, 15])
    ckT = din("ckT", [512, PAST])
    cv = din("cv", [PAST, 512])
    clf = din("clf", [PAST, 8])
    cmkT = din("cmkT", [2, 512, 256])
    cmv = din("cmv", [2, 256, 512])
    w_in = din("w_in", [2, D, D])
    w_out = din("w_out", [2, D, D])
    w_pool = din("w_pool", [4, 128, 128])
    w_kvf = din("w_kvf", [D, 1032])
    w_mem = din("w_mem", [2, D, D])
    w_up = din("w_up", [2, D, 4096])
    w_down = din("w_down", [2, 4096, D])
    gains = din("gains", [128, NG])
    pscale = din("pscale", [128, 4])
    bfb = din("bfb", [128, 32])
    flag = din("flag", [128, 256], I32)
    masks = din("masks", [8, 128, TT], BF16)
    mask_s = din("mask_s", [128, TT], BF16)
    cntfix = din("cntfix", [128, 4, 16])
    tri = din("tri", [128, 128])
    lastrow = din("lastrow", [128, 128])
    lastrow16 = din("lastrow16", [128, 128])
    pmask = din("pmask", [128, 2])
    rflag = din("rflag", [128, 2])
    ident_in = din("ident", [128, 128], BF16)

    yT_own = dout("yT_own", [D, SEQ // 2])
    ysT = dout("ysT", [D, ST])
    pstate_p = dout("pstate_p", [512, 15])
    pstate_s = dout("pstate_s", [512, 15])
    kT_p = dout("kT_p", [512, SEQ])
    v_p = dout("v_p", [SEQ, 512])
    logf_p = dout("logf_p", [SEQ, 8])
    kT_s = dout("kT_s", [512, ST])
    v_s = dout("v_s", [ST, 512])
    logf_s = dout("logf_s", [ST, 8])
    mkT_p = dout("mkT_p", [2, 512, 256])
    mv_p = dout("mv_p", [2, 256, 512])

    win_bf = dint("win_bf", [2, D, D])
    wout_bf = dint("wout_bf", [2, D, D])
    wpool_bf = dint("wpool_bf", [4, 128, 128])
    wkvf_bf = dint("wkvf_bf", [D, 1032])
    wmem_bf = dint("wmem_bf", [2, D, D])
    wup_bf = dint("wup_bf", [2, D, 4096])
    wdown_bf = dint("wdown_bf", [2, 4096, D])
    kbf = dint("kbf", [512, SEQ])
    vbf = dint("vbf", [SEQ, 520])
    kbf_s = dint("kbf_s", [512, PAST + ST])
    vbf_s = dint("vbf_s", [PAST + ST, 520])
    x1own = dint("x1own", [D, SEQ // 2], F32)
    x1s = dint("x1s", [D, ST], F32)

    S = Sched(nc)
    with ExitStack() as st:
        S.open(st)

        def sb(name, shape, dt):
            return st.enter_context(nc.sbuf_tensor(name, list(shape), dt))

        xbufs = [sb("xbufs[X.i]%d" % i, [128, KC, TT], F32) for i in range(2)]
        r_xs = [[Res("x%d_%d" % (i, c)) for c in range(KC)] for i in range(2)]

        class _X:
            i = 0
        X = _X()
        hbuf = sb("hbuf", [128, KC, TT], BF16);     r_h = [Res("h%d" % c) for c in range(KC)]
        sqatt = sb("sqatt", [128, KC, TT], BF16)
        r_sqc = [Res("sq%d" % c) for c in range(KC)]
        r_att = r_sqc
        if os.environ.get("KPAD"):
            _pad = sb("padz", [128, int(os.environ["KPAD"])], F32)
        zbuf = sb("zbuf", [128, KC, TT], F32);      r_z = [Res("z%d" % c) for c in range(KC)]
        abuf = sb("abuf", [128, 32 * 520], BF16)
        r_a = [Res("a%d" % g) for g in range(8)]
        a_v = abuf[:, 0:32 * TT].rearrange("p (c t) -> p c t", t=TT)
        vaug_v = abuf[:, :].rearrange("p (b h e) -> p b h e", h=8, e=65)
        qm = sb("qm", [128, 4, TT], BF16);          r_qm = [Res("qm%d" % c) for c in range(4)]
        wts = [sb("wt%d" % i, [128, 8, TT], BF16) for i in range(3)]
        wring = Ring([(w, Res("wt%d" % i)) for i, w in enumerate(wts)])
        wf_sb = sb("wf_sb", [128, 8, 8], BF16); r_wf = Res("wf")
        rtmp = Ring([(sb("rt%d" % i, [128, TT], BF16), Res("rt%d" % i)) for i in range(3)])
        ptmp = Ring([(sb("pt%d" % i, [128, TT], BF16), Res("pt%d" % i)) for i in range(4)])
        lnv = sb("lnv", [128, TT], F32);            r_ln = Res("ln")
        rstd = sb("rstd", [128, TT], F32);          r_rstd = Res("rstd")
        rden = sb("rden", [128, TT], F32);          r_rden = Res("rden")
        rstd2 = sb("rstd2", [128, TT], F32);        r_rstd2 = Res("rstd2")
        uni = sb("uni", [128, 8 * TT + 2 * 4096], BF16)
        u_v = uni[:, 0:2 * 4 * 528].bitcast(F32).rearrange("p (c w) -> p c w", w=528)
        o1 = 2 * 4 * 528
        sA = uni[:, o1:o1 + 2 * 528].bitcast(F32)
        sB = uni[:, o1 + 2 * 528:o1 + 4 * 528].bitcast(F32)
        o2 = o1 + 4 * 528
        pooled = uni[:, o2:o2 + 4 * TT].rearrange("p (c t) -> p c t", t=TT)
        r_u = Res("u"); r_sA = Res("sA"); r_sB = Res("sB"); r_pooled = [Res("pl%d" % g) for g in range(4)]
        qaug = uni[:, 0:8 * TT].rearrange("p (h t) -> p h t", t=TT)
        r_qaug = [Res("qa%d" % h) for h in range(8)]
        khs = [uni[:, 8 * TT:8 * TT + 4096], uni[:, 8 * TT + 4096:8 * TT + 8192]]
        khring = Ring([(khs[0], Res("kh0")), (khs[1], Res("kh1"))])
        attnh = sb("attnh", [128, 8, TT], BF16);    r_attnh = [Res("ah%d" % h) for h in range(8)]
        memk = [sb("memk%d" % i, [128, 4, 256], BF16) for i in range(2)]
        memv = [sb("memv%d" % i, [128, 2, 512], BF16) for i in range(2)]
        r_memk = [Res("mk0"), Res("mk1")]
        r_memv = [Res("mv0"), Res("mv1")]
        vf = Ring([(sb("vf%d" % i, [128, 512], F32), Res("vf%d" % i)) for i in range(2)])
        vst = [sb("vst%d" % i, [128, 8, 65], BF16) for i in range(2)]
        vstr = Ring([(vst[0], Res("vst0")), (vst[1], Res("vst1"))])
        lfr = Ring([(sb("lf%d" % i, [128, 32], F32), Res("lf%d" % i)) for i in range(2)])
        fl = sb("fl", [128, 32], F32); r_fl = Res("fl")
        fe = sb("fe", [128, 32], F32); r_fe = Res("fe")
        Call = sb("Call", [128, NBLK, 8], F32);     r_Call = [Res("C%d" % b) for b in range(NBLK)]
        Cb_all = sb("Cb_all", [128, NBLK, 8], F32); r_Cb = [Res("Cb%d" % b) for b in range(NBLK)]
        Cb_own = sb("Cb_own", [128, NBLK // 2, 8], F32); r_Cbo = [Res("Cbo%d" % b) for b in range(NBLK // 2)]
        Call_s = sb("Call_s", [128, NSB + 1, 8], F32); r_Calls = [Res("Cs%d" % b) for b in range(NSB + 1)]
        Cb_s = sb("Cb_s", [128, 8], F32); r_Cbs = Res("Cbs")
        clf_sb = sb("clf_sb", [128, NSB, 8], F32); r_clf = Res("clf")
        zeros8 = sb("zeros8", [128, 8], F32); r_zeros8 = Res("z8")
        biasK = sb("biasK", [128, NBLK, 8], F32); r_biasK = Res("biasK")
        offt = sb("offt", [128, 4, 8], F32); r_offt = Res("offt")
        offhi = sb("offhi", [128, 4, 8], BF16); r_offhi = Res("offhi")
        offmid = sb("offmid", [128, 4, 8], BF16); r_offmid = Res("offmid")
        src2 = sb("src2", [128, 4, 8], F32); r_src2 = Res("src2")
        tmp2 = sb("tmp2", [128, 4, 8], F32); r_tmp2 = Res("tmp2")
        zrow = sb("zrow", [128, 128], F32); r_zrow = Res("zrow")
        xown = abuf[:, 0:8 * TT].bitcast(F32).rearrange("p (c t) -> p c t", t=256)
        ones_bf = sb("ones_bf", [128, 128], BF16)
        onesT = sb("onesT", [128, TT], BF16)
        rflag_sb = sb("rflag_sb", [128, 2], F32)
        onesf = sb("onesf", [128, 128], F32)
        ident = sb("ident_sb", [128, 128], BF16)
        tri_sb = sb("tri_sb", [128, 128], F32)
        lastrow_sb = sb("lastrow_sb", [128, 128], F32)
        lastrow16_sb = sb("lastrow16_sb", [128, 128], F32)
        g32 = sb("g32", [128, NG], F32)
        graw = sb("graw", [128, NG], F32)
        pscale_sb = sb("pscale_sb", [128, 4], F32)
        bfb_sb = sb("bfb_sb", [128, 32], F32)
        flag_sb = sb("flag_sb", [128, 256], I32)
        masks_sb = sb("masks_sb", [128, 8, TT], BF16)
        mask_s_sb = sb("mask_s_sb", [128, TT], BF16)
        cntfix_sb = sb("cntfix_sb", [128, 4, 16], F32)
        pmask_sb = sb("pmask_sb", [128, 2], F32)
        wpool_sb = sb("wpool_sb", [128, 4, 128], BF16)
        r_const = Res("const")
        r_g32 = Res("g32")
        r_wpool = Res("wpool")

        pss = [st.enter_context(nc.psum_tensor("ps%d" % i, [128, 512], F32)) for i in range(8)]
        psring = Ring([(p, Res("ps%d" % i)) for i, p in enumerate(pss[0:4])])
        accring = Ring([(p, Res("acc%d" % i)) for i, p in enumerate(pss[4:8])])

        def mm(out_ap, pairs, reads, writes, first=True, last=True):
            def emit(e, pairs=pairs, out_ap=out_ap):
                n = len(pairs)
                for i, (l, r) in enumerate(pairs):
                    ins = e.matmul(out_ap, lhsT=l, rhs=r, start=(first and i == 0), stop=(last and i == n - 1))
                return ins
            S.op("pe", emit, reads, writes)

        def act(func, out, in_, reads, writes, **kw):
            S.op("act", lambda e: e.activation(out=out, in_=in_, func=func, **kw), reads, writes)

        def act_copy(out, in_, reads, writes):
            if out.dtype == BF16:
                act(AF.Copy, out, in_, reads, writes)
            else:
                np_ = in_.shape[0]
                p0 = 64 if np_ == 1 else 0
                act(AF.Copy, out, in_, reads + [r_const], writes, scale=onesf[p0:p0 + np_, 0:1])

        def dve_tt(out, in0, in1, op, reads, writes, eng="dve"):
            S.op(eng, lambda e: e.tensor_tensor(out=out, in0=in0, in1=in1, op=op), reads, writes)

        def dve_stt(out, in0, scalar, in1, op0, op1, reads, writes):
            S.op("dve", lambda e: e.scalar_tensor_tensor(out=out, in0=in0, scalar=scalar, in1=in1, op0=op0, op1=op1),
                 reads, writes)

        def dve_ts(out, in0, s1, s2, op0, op1, reads, writes, eng="dve"):
            S.op(eng, lambda e: e.tensor_scalar(out=out, in0=in0, scalar1=s1, scalar2=s2, op0=op0, op1=op1),
                 reads, writes)

        def dve_copy(out, in_, reads, writes, eng="dve"):
            if out.dtype == BF16:
                S.op(eng, lambda e: e.tensor_copy(out=out, in_=in_), reads, writes)
            else:
                S.op(eng, lambda e: e.tensor_scalar(out=out, in0=in_, scalar1=1.0, scalar2=None, op0=ALU.mult,
                                                    op1=ALU.bypass), reads, writes)

        def memset(ap, val, writes, eng="dve"):
            S.op(eng, lambda e: e.memset(ap, val), (), writes)

        def load_w(src2d, k0, nk, n0, nn, rsrc, q="sp"):
            wt, rw = wring.next()
            S.dma(q, wt[:, 0:nk, 0:nn],
                  src2d[k0 * 128:(k0 + nk) * 128, n0:n0 + nn].rearrange("(c p) n -> p c n", p=128),
                  reads=rsrc, writes=[rw])
            return wt, rw

        memset(ones_bf[:], 1.0, [r_const])
        memset(onesT[:], 1.0, [r_const])
        memset(onesf[:], 1.0, [r_const])
        memset(zeros8[:], 0.0, [r_zeros8])
        memset(zrow[:], 0.0, [r_zrow])
        memset(Call_s[:, NSB, :], 0.0, [r_Calls[NSB]])
        for v in vst:
            memset(v[:], 1.0, [r_const])
        for (dst, src) in ((graw, gains), (pscale_sb, pscale), (bfb_sb, bfb), (flag_sb, flag), (mask_s_sb, mask_s),
                           (cntfix_sb, cntfix), (tri_sb, tri), (lastrow_sb, lastrow), (lastrow16_sb, lastrow16),
                           (pmask_sb, pmask), (ident, ident_in), (rflag_sb, rflag)):
            S.dma("sp", dst[:], src, writes=[r_const], merge=True)
        S.dma("sp", masks_sb[:], masks.rearrange("m p t -> p m t"), writes=[r_const], merge=True)
        dve_ts(g32[:], graw[:], 32.0, None, ALU.mult, ALU.bypass, [r_const], [r_g32])

        rW = {}
        SK = os.environ.get("KSKIP", "")

        cast_hist = []
        cast_todo = []

        def cast_piece(name, l, dst2d, src2d, extra=()):
            r = Res(name + str(l))
            rW.setdefault((name, l), []).append(r)
            prev = [cast_hist[-2]] if len(cast_hist) >= 2 else []
            S.dma("pool", dst2d, src2d, reads=prev + list(extra), writes=[r])
            cast_hist.append(r)

        def flat(ap, cols):
            return ap.rearrange("r (a n) -> (r a) n", n=cols)

        def cast_w(name, l, dst, src, cols, defer=False, rows_per=512):
            d2, s2 = flat(dst, cols), flat(src, cols)
            if not defer:
                cast_piece(name, l, d2, s2)
                return
            rW.setdefault((name, l), [])
            n = d2.shape[0]
            for r0 in range(0, n, rows_per):
                cast_todo.append((name, l, d2[r0:r0 + rows_per, :], s2[r0:r0 + rows_per, :]))

        def cast_some(k, extra=()):
            for _ in range(k):
                if cast_todo:
                    cast_piece(*cast_todo.pop(0), extra=extra)

        cast_w("wmem", 0, wmem_bf[0], w_mem[0], 1024)
        cast_w("win", 0, win_bf[0], w_in[0], 1024)
        rW[("wpool", 0)] = [Res("wpool")]
        S.dma("pool", wpool_bf.rearrange("g c d -> (g c) d"), w_pool.rearrange("g c d -> (g c) d"),
              writes=rW[("wpool", 0)])
        cast_w("wout", 0, wout_bf[0], w_out[0], 1024)
        cast_w("wup", 0, wup_bf[0], w_up[0], 1024, defer=True, rows_per=4096)
        cast_w("wdown", 0, wdown_bf[0], w_down[0], 1024, defer=True, rows_per=4096)
        cast_w("wkvf", 0, wkvf_bf, w_kvf, 1032, defer=True, rows_per=1024)
        cast_w("wmem", 1, wmem_bf[1], w_mem[1], 1024, defer=True)
        cast_w("win", 1, win_bf[1], w_in[1], 1024, defer=True)
        cast_w("wout", 1, wout_bf[1], w_out[1], 1024, defer=True)
        cast_w("wup", 1, wup_bf[1], w_up[1], 1024, defer=True)
        cast_w("wdown", 1, wdown_bf[1], w_down[1], 1024, defer=True)
        r_kbfs = Res("kbfs")
        r_vbfs = Res("vbfs")

        LN_DEPS = float(np.log(D * EPS))

        def sumsq_rstd(src_tile, nch, T, src_res, want_b=False):
            hh = nch // 2
            for a in range(2):
                S.op("act", lambda e, a=a: e.activation(out=sqatt[:, a * hh:(a + 1) * hh, 0:T],
                                                        in_=src_tile[:, a * hh:(a + 1) * hh, 0:T], func=AF.Square),
                     src_res[a * hh:(a + 1) * hh], r_sqc[a * hh:(a + 1) * hh])
            ps, rp = psring.next()
            mm(ps[:, 0:T], [(ones_bf[:], sqatt[:, c, 0:T]) for c in range(nch)], r_sqc[0:nch] + [r_const], [rp])
            act(AF.Ln, lnv[:, 0:T], ps[:, 0:T], [rp], [r_ln], bias=float(D * EPS))
            act(AF.Exp, rstd[:, 0:T], lnv[:, 0:T], [r_ln], [r_rstd], scale=-0.5)
            if want_b:
                act(AF.Exp, rstd2[:, 0:T], lnv[:, 0:T], [r_ln], [r_rstd2], scale=2.0, bias=LN_DEPS)

        def prep_xg(src_tile, src_res, gcol, T, want_b=False):
            for c in (0, 1, 3, 4, 6, 7, 2, 5):
                S.op("act", lambda e, c=c: e.activation(out=hbuf[:, c, 0:T], in_=src_tile[:, c, 0:T], func=AF.Copy,
                                                        scale=g32[:, gcol + c:gcol + c + 1]),
                     [src_res[c], r_g32], [r_h[c]])
            sumsq_rstd(src_tile, KC, T, src_res, want_b=want_b)

        def evac_post(oc, ps_ap, rp, gcol, T):
            S.op("act", lambda e: e.activation(out=zbuf[:, oc, 0:T], in_=ps_ap, func=AF.Copy,
                                               scale=g32[:, gcol + oc:gcol + oc + 1]), [rp, r_g32], [r_z[oc]])
            S.op("act", lambda e: e.activation(out=hbuf[:, oc, 0:T], in_=ps_ap, func=AF.Square), [rp], [r_h[oc]])

        def post_norm_residual(T, bias_tile=None, bias_res=None):
            ps, rp = psring.next()
            mm(ps[:, 0:T], [(ones_bf[:], hbuf[:, c, 0:T]) for c in range(KC)], r_h + [r_const], [rp])
            if bias_tile is None:
                act(AF.Ln, lnv[:, 0:T], ps[:, 0:T], [rp], [r_ln], bias=float(D * EPS))
            else:
                dve_tt(lnv[:, 0:T], ps[:, 0:T], bias_tile[:, 0:T], ALU.add, [rp, bias_res], [r_ln])
                act(AF.Ln, lnv[:, 0:T], lnv[:, 0:T], [r_ln], [r_ln])
            act(AF.Exp, rstd[:, 0:T], lnv[:, 0:T], [r_ln], [r_rstd], scale=-0.5)
            xb, rx = xbufs[X.i], r_xs[X.i]
            for c in (2, 5, 0, 1, 3, 4, 6, 7):
                dve_tt(zbuf[:, c, 0:T], zbuf[:, c, 0:T], rstd[:, 0:T], ALU.mult, [r_z[c], r_rstd], [r_z[c]])
                dve_tt(xb[:, c, 0:T], xb[:, c, 0:T], zbuf[:, c, 0:T], ALU.add, [rx[c], r_z[c]], [rx[c]],
                       eng=("pool" if c % 3 == 2 else "dve"))

        def rms_prep(src_tile, src_res, gcol, T):
            sumsq_rstd(src_tile, KC, T, src_res)
            for c in range(KC):
                dve_stt(hbuf[:, c, 0:T], src_tile[:, c, 0:T], g32[:, gcol + c:gcol + c + 1], rstd[:, 0:T],
                        ALU.mult, ALU.mult, [src_res[c], r_rstd, r_g32], [r_h[c]])

        def mem_project(l, sq="pool"):
            T = 256
            S.dma("sp", xbufs[X.i][:, :, 0:T], memT.rearrange("(c p) t -> p c t", p=128), writes=r_xs[X.i])
            rms_prep(xbufs[X.i], r_xs[X.i], G_MEM + 8 * l, T)
            wg, rw = load_w(wmem_bf[l], 0, 8, 0, 512, rW[("wmem", l)])
            for hm in range(4):
                ps, rp = psring.next()
                mm(ps[:, 0:T], [(wg[:, kc, hm * 128:(hm + 1) * 128], hbuf[:, kc, 0:T]) for kc in range(KC)],
                   r_h + [rw], [rp])
                act_copy(zbuf[:, hm, 0:T], ps[:, 0:T], [rp], [r_z[hm]])
                dve_copy(memk[0][:, hm, :], zbuf[:, hm, 0:T], [r_z[hm]], [r_memk[0]])
            S.dma(sq, mkT_p[l].rearrange("(h p) t -> p h t", p=128), zbuf[:, 0:4, 0:T], reads=r_z[0:4])
            wg, rw = load_w(wmem_bf[l], 0, 8, 512, 512, rW[("wmem", l)])
            for kb in range(2):
                ps, rp = psring.next()
                mm(ps[:, :], [(hbuf[:, kc, kb * 128:(kb + 1) * 128], wg[:, kc, :]) for kc in range(KC)],
                   r_h + [rw], [rp])
                act_copy(zbuf[:, 4 + kb, :], ps[:, :], [rp], [r_z[4 + kb]])
                dve_copy(memv[0][:, kb, :], zbuf[:, 4 + kb, :], [r_z[4 + kb]], [r_memv[0]])
            S.dma(sq, mv_p[l].rearrange("(b p) n -> p b n", p=128), zbuf[:, 4:6, :], reads=r_z[4:6])

        def mem_load_sample(l):
            S.dma("sp", zbuf[:, 0:4, 0:256], cmkT[l].rearrange("(h p) t -> p h t", p=128), writes=r_z[0:4])
            dve_copy(memk[1][:, :, :], zbuf[:, 0:4, 0:256], r_z[0:4], [r_memk[1]])
            S.dma("sp", zbuf[:, 4:6, :], cmv[l].rearrange("(b p) n -> p b n", p=128), writes=r_z[4:6])
            dve_copy(memv[1][:, :, :], zbuf[:, 4:6, :], r_z[4:6], [r_memv[1]])

        def mem_attention(kind, T):
            MK, MV = memk[kind], memv[kind]

            def issue_S(hm):
                out = []
                for kb in range(2):
                    s_ps, r_s = psring.next()
                    mm(s_ps[:, 0:T], [(MK[:, hm, kb * 128:(kb + 1) * 128], qm[:, hm, 0:T])],
                       [r_memk[kind], r_qm[hm]], [r_s])
                    out.append((s_ps, r_s))
                return out
            nxt = issue_S(0)
            for hm in range(4):
                cur = nxt
                pts = []
                for kb in range(2):
                    s_ps, r_s = cur[kb]
                    pt, r_pt = ptmp.next()
                    act(AF.Exp, pt[:, 0:T], s_ps[:, 0:T], [r_s], [r_pt], scale=float(128 ** -0.5))
                    pts.append((pt, r_pt))
                if hm + 1 < 4:
                    nxt = issue_S(hm + 1)
                o_ps, r_o = accring.next()
                d_ps, r_d = accring.next()
                for kb in range(2):
                    pt, r_pt = pts[kb]
                    mm(o_ps[:, 0:T], [(MV[:, kb, hm * 128:(hm + 1) * 128], pt[:, 0:T])], [r_memv[kind], r_pt], [r_o],
                       first=(kb == 0), last=(kb == 1))
                    mm(d_ps[:, 0:T], [(ones_bf[:], pt[:, 0:T])], [r_const, r_pt], [r_d],
                       first=(kb == 0), last=(kb == 1))
                act(AF.Ln, lnv[:, 0:T], d_ps[:, 0:T], [r_d], [r_ln])
                act(AF.Exp, rden[:, 0:T], lnv[:, 0:T], [r_ln], [r_rden], scale=-1.0)
                dve_tt(sqatt[:, 4 + hm, 0:T], o_ps[:, 0:T], rden[:, 0:T], ALU.mult, [r_o, r_rden], [r_att[4 + hm]])

        def mlp(l, T):
            prep_xg(xbufs[X.i], r_xs[X.i], G_MLP_PRE + 8 * l, T, want_b=True)
            for og in range(8):
                if og == 4:
                    cast_some(1)
                wg, rw = load_w(wup_bf[l], 0, 8, og * 512, 512, rW[("wup", l)])
                for j in range(4):
                    ps, rp = psring.next()
                    mm(ps[:, 0:T], [(wg[:, kc, j * 128:(j + 1) * 128], hbuf[:, kc, 0:T]) for kc in range(KC)],
                       r_h + [rw], [rp])
                    rt, r_rt = rtmp.next()
                    act(AF.Relu, rt[:, 0:T], ps[:, 0:T], [rp], [r_rt])
                    dve_tt(a_v[:, og * 4 + j, 0:T], rt[:, 0:T], rt[:, 0:T], ALU.mult, [r_rt], [r_a[og]])
            for half in range(2):
                if half == 1:
                    cast_some(1)
                accs = [(accring if half == 0 else psring).next() for _ in range(4)]
                for kg in range(4):
                    wg, rw = load_w(wdown_bf[l], kg * 8, 8, half * 512, 512, rW[("wdown", l)])

                    def emit(e, wg=wg, kg=kg, accs=accs):
                        for j in range(4):
                            for kc in range(8):
                                ins = e.matmul(accs[j][0][:, 0:T], lhsT=wg[:, kc, j * 128:(j + 1) * 128],
                                               rhs=a_v[:, kg * 8 + kc, 0:T],
                                               start=(kg == 0 and kc == 0), stop=(kg == 3 and kc == 7))
                        return ins
                    if kg < 3:
                        S.op("pe", emit, r_a[2 * kg:2 * kg + 2] + [rw], [a[1] for a in accs])
                    else:
                        for j in range(4):
                            def emit_j(e, wg=wg, kg=kg, accs=accs, j=j):
                                for kc in range(8):
                                    ins = e.matmul(accs[j][0][:, 0:T], lhsT=wg[:, kc, j * 128:(j + 1) * 128],
                                                   rhs=a_v[:, kg * 8 + kc, 0:T], start=False, stop=(kc == 7))
                                return ins
                            S.op("pe", emit_j, r_a[2 * kg:2 * kg + 2] + [rw], [accs[j][1]])
                            evac_post(half * 4 + j, accs[j][0][:, 0:T], accs[j][1], G_MLP_POST + 8 * l, T)
            post_norm_residual(T, bias_tile=rstd2, bias_res=r_rstd2)

        def kvf(kind, t, T):
            rms_prep(xbufs[X.i], r_xs[X.i], G_KV, T)
            wf, rwf = wf_sb, r_wf
            S.dma("sp", wf_sb[:], wkvf_bf[:, 1024:1032].rearrange("(c p) n -> p c n", p=128), reads=rW[("wkvf", 0)],
                  writes=[r_wf])
            nb = max(1, T // 128)
            tn = min(128, T)
            W8 = nb * 8
            psf, rpf = psring.next()
            for b in range(nb):
                tok = slice(b * 128, b * 128 + tn)
                mm(psf[0:tn, b * 8:(b + 1) * 8], [(hbuf[:, kc, tok], wf[:, kc, 0:8]) for kc in range(KC)],
                   r_h + [rwf], [rpf])
            dve_tt(fl[0:tn, 0:W8], psf[0:tn, 0:W8], bfb_sb[0:tn, 0:W8], ALU.add, [rpf, r_const], [r_fl])
            act(AF.Exp, fe[0:tn, 0:W8], fl[0:tn, 0:W8], [r_fl], [r_fe], scale=-1.0)
            act(AF.Ln, fl[0:tn, 0:W8], fe[0:tn, 0:W8], [r_fe], [r_fl], bias=1.0)
            lf, r_lf = lfr.next()
            dve_ts(lf[0:tn, 0:W8], fl[0:tn, 0:W8], -1.0, None, ALU.mult, ALU.bypass, [r_fl], [r_lf])
            if kind == 0:
                blk0 = t * 4
                S.dma("pool", logf_p[t * TT:(t + 1) * TT, :].rearrange("(b p) h -> p b h", p=128),
                      lf[:, 0:W8].rearrange("p (b h) -> p b h", h=8), reads=[r_lf])
                prevC = Call[:, blk0 - 1, :] if blk0 > 0 else zeros8[:]
                r_prev = r_Call[blk0 - 1] if blk0 > 0 else r_zeros8
            else:
                S.dma("pool", logf_s[0:tn, :], lf[0:tn, 0:8], reads=[r_lf])
                prevC, r_prev = Call_s[:, NSB - 1, :], r_Calls[NSB - 1]
            wg, rw = load_w(wkvf_bf, 0, 8, 0, 512, rW[("wkvf", 0)])
            for hd in range(8):
                ps, rp = psring.next()
                mm(ps[0:64, 0:T], [(wg[:, kc, hd * 64:(hd + 1) * 64], hbuf[:, kc, 0:T]) for kc in range(KC)],
                   r_h + [rw], [rp])
                act_copy(zbuf[0:64, hd, 0:T], ps[0:64, 0:T], [rp], [r_z[hd]])
            if kind == 0:
                c0 = t * TT
                S.dma("pool", kT_p[:, c0:c0 + T].rearrange("(h d) t -> d h t", d=64), zbuf[0:64, :, 0:T], reads=r_z)
                S.dma("pool", kbf[:, c0:c0 + T].rearrange("(h d) t -> d h t", d=64), zbuf[0:64, :, 0:T], reads=r_z,
                      writes=[r_kbf])
            else:
                S.dma("pool", kT_s.rearrange("(h d) t -> d h t", d=64), zbuf[0:64, :, 0:T], reads=r_z)
                S.dma("pool", kbf_s[:, PAST:PAST + T].rearrange("(h d) t -> d h t", d=64), zbuf[0:64, :, 0:T],
                      reads=r_z, writes=[r_kbfs])
            wv, rwv = load_w(wkvf_bf, 0, 8, 512, 512, rW[("wkvf", 0)])
            for b in range(nb):
                tn = min(128, T)
                tok = slice(b * 128, b * 128 + tn)
                ps, rp = psring.next()
                mm(ps[0:tn, :], [(hbuf[:, kc, tok], wv[:, kc, :]) for kc in range(KC)],
                   r_h + [rwv], [rp])
                vft, r_vf = vf.next()
                act_copy(vft[0:tn, :], ps[0:tn, :], [rp], [r_vf])
                vs, r_vs = vstr.next()
                dve_copy(vs[0:tn, :, 0:64], vft[0:tn, :].rearrange("p (h e) -> p h e", e=64), [r_vf], [r_vs])
                if kind == 0:
                    row0 = t * TT + b * 128
                    S.dma("pool", v_p[row0:row0 + tn, :], vft[0:tn, :], reads=[r_vf])
                    S.dma("pool", vbf[row0:row0 + tn, :], vs[0:tn, :, :].rearrange("p h e -> p (h e)"), reads=[r_vs],
                          writes=[r_vbf])
                else:
                    S.dma("pool", v_s[0:tn, :], vft[0:tn, :], reads=[r_vf])
                    S.dma("pool", vbf_s[PAST:PAST + tn, :], vs[0:tn, :, :].rearrange("p h e -> p (h e)"),
                          reads=[r_vs], writes=[r_vbfs])
            psc, rpc = psring.next()
            for b in range(nb):
                def emit_c(e, b=b):
                    o = psc[0:tn, b * 8:(b + 1) * 8]
                    e.matmul(o, lhsT=tri_sb[0:tn, 0:tn], rhs=lf[0:tn, b * 8:(b + 1) * 8], start=True, stop=False)
                    for b2 in range(b):
                        e.matmul(o, lhsT=onesf[:, 0:tn], rhs=lf[:, b2 * 8:(b2 + 1) * 8], start=False, stop=False)
                    return e.matmul(o, lhsT=lastrow_sb[:, 0:tn], rhs=prevC, start=False, stop=True)
                S.op("pe", emit_c, [r_lf, r_prev, r_const], [rpc])

                def emit_b(e, b=b):
                    o = psc[:, 32 + b * 8:32 + (b + 1) * 8]
                    for b2 in range(b + 1):
                        e.matmul(o, lhsT=onesf[0:tn, :], rhs=lf[0:tn, b2 * 8:(b2 + 1) * 8], start=(b2 == 0), stop=False)
                    return e.matmul(o, lhsT=lastrow_sb[:, :], rhs=prevC, start=False, stop=True)
                S.op("pe", emit_b, [r_lf, r_prev, r_const], [rpc])
            if kind == 0:
                dve_copy(Call[:, blk0:blk0 + nb, :], psc[:, 0:W8].rearrange("p (b h) -> p b h", h=8), [rpc],
                         r_Call[blk0:blk0 + nb])
                dve_copy(Cb_all[:, blk0:blk0 + nb, :], psc[:, 32:32 + W8].rearrange("p (b h) -> p b h", h=8), [rpc],
                         r_Cb[blk0:blk0 + nb])
                for j in (2 * t, 2 * t + 1):
                    dve_ts(Cb_own[:, j, :], Cb_all[:, 2 * j, :], rflag_sb[:, 1:2], None, ALU.mult, ALU.bypass,
                           [r_Cb[2 * j], r_const], [r_Cbo[j]])
                    dve_stt(Cb_own[:, j, :], Cb_all[:, 2 * j + 1, :], rflag_sb[:, 0:1], Cb_own[:, j, :], ALU.mult, ALU.add,
                            [r_Cb[2 * j + 1], r_Cbo[j], r_const], [r_Cbo[j]])
            else:
                dve_copy(Call_s[0:tn, NSB, :], psc[0:tn, 0:8], [rpc], [r_Calls[NSB]])
                dve_copy(Cb_s[:], psc[:, 32:40], [rpc], [r_Cbs])

        def cum_block(lf, r_lf, tn, prevC, r_prev, dstC, r_dst):
            ps, rp = psring.next()

            def emit(e):
                e.matmul(ps[0:tn, 0:8], lhsT=tri_sb[0:tn, 0:tn], rhs=lf[0:tn, :], start=True, stop=False)
                return e.matmul(ps[0:tn, 0:8], lhsT=lastrow_sb[:, 0:tn], rhs=prevC, start=False, stop=True)
            S.op("pe", emit, [r_lf, r_prev, r_const], [rp])
            dve_copy(dstC[0:tn] if tn < 128 else dstC, ps[0:tn, 0:8], [rp], [r_dst])

        def bcast_last(srcC, r_src, tn, dst, r_dst):
            ps, rp = psring.next()
            lr = lastrow_sb if tn == 128 else lastrow16_sb
            mm(ps[:, 0:8], [(lr[0:tn, :], srcC[0:tn] if tn < 128 else srcC)], [r_src, r_const], [rp])
            dve_copy(dst, ps[:, 0:8], [rp], [r_dst])

        r_kbf = Res("kbf")
        r_vbf = Res("vbf")
        r_x1own = Res("x1own")
        r_x1s = Res("x1s")

        pre = {"key": None, "buf": 0}

        def issue_x_load(key, bufi):
            layer, kind, t = key
            T = TT if kind == 0 else ST
            if layer == 0:
                src, rd = (xT[:, t * TT:t * TT + T], []) if kind == 0 else (xsT, [])
            else:
                src, rd = (x1own[:, t * TT:t * TT + T], [r_x1own]) if kind == 0 else (x1s, [r_x1s])
            S.dma("sp", xbufs[bufi][:, :, 0:T], src.rearrange("(c p) t -> p c t", p=128), reads=rd, writes=r_xs[bufi])

        def begin_tile(key):
            if pre["key"] == key:
                X.i = pre["buf"]
            else:
                issue_x_load(key, X.i)
            pre["key"] = None

        def prefetch_x(key):
            if key is None:
                return
            b = 1 - X.i
            issue_x_load(key, b)
            pre["key"], pre["buf"] = key, b

        def layer0_tile(kind, t, T, nxt=None):
            W = 15 + T
            begin_tile((0, kind, t))
            xi = X.i
            if kind == 0:
                if t == 0:
                    memset(u_v[:, :, 0:15], 0.0, [r_u])
            else:
                S.dma("sp", u_v[:, :, 0:15], poolprev.rearrange("(c p) t -> p c t", p=128), writes=[r_u])
            prep_xg(xbufs[X.i], r_xs[X.i], G_MIX_PRE, T)
            wg, rw = load_w(win_bf[0], 0, 8, 0, 512, rW[("win", 0)])
            for oc in range(4):
                ps, rp = psring.next()
                mm(ps[:, 0:T], [(wg[:, kc, oc * 128:(oc + 1) * 128], hbuf[:, kc, 0:T]) for kc in range(KC)],
                   r_h + [rw], [rp])
                dve_tt(u_v[:, oc, 15:W], ps[:, 0:T], rstd[:, 0:T], ALU.mult, [rp, r_rstd], [r_u])
            wg, rw = load_w(win_bf[0], 0, 8, 512, 512, rW[("win", 0)])
            for hm in range(4):
                ps, rp = psring.next()
                mm(ps[:, 0:T], [(wg[:, kc, hm * 128:(hm + 1) * 128], hbuf[:, kc, 0:T]) for kc in range(KC)],
                   r_h + [rw], [rp])
                dve_tt(qm[:, hm, 0:T], ps[:, 0:T], rstd[:, 0:T], ALU.mult, [rp, r_rstd], [r_qm[hm]])
            if kind == 0 and t == 0:
                cast_some(3, extra=[rw] + r_xs[xi])
            else:
                cast_some(1)
            if kind == 0 and t == 0:
                S.dma("sp", wpool_sb[:], wpool_bf.rearrange("g c d -> c g d"), reads=rW[("wpool", 0)],
                      writes=[r_wpool])
            yield "head"
            X.i = xi
            prefetch_x(nxt)
            if kind == 0 and t == NT0 - 2:
                sample_cache_prep()
            for g in range(4):
                w = 2 << g
                ug = u_v[:, g, :]
                dve_tt(sA[:, 1:W], ug[:, 1:W], ug[:, 0:W - 1], ALU.add, [r_u], [r_sA])
                sw, r_sw = sA, r_sA
                if g >= 1:
                    dve_tt(sB[:, 3:W], sA[:, 3:W], sA[:, 1:W - 2], ALU.add, [r_sA], [r_sB])
                    sw, r_sw = sB, r_sB
                if g >= 2:
                    dve_tt(sA[:, 7:W], sB[:, 7:W], sB[:, 3:W - 4], ALU.add, [r_sB], [r_sA])
                    sw, r_sw = sA, r_sA
                if g >= 3:
                    dve_tt(sB[:, 15:W], sA[:, 15:W], sA[:, 7:W - 8], ALU.add, [r_sA], [r_sB])
                    sw, r_sw = sB, r_sB
                if kind == 0 and t == 0:
                    dve_tt(sw[:, 15:31], sw[:, 15:31], cntfix_sb[:, g, :], ALU.mult, [r_sw, r_const], [r_sw])
                dve_stt(pooled[:, g, 0:T], sw[:, 15:W], 1.0 / w, ug[:, 15:W], ALU.mult, ALU.subtract,
                        [r_sw, r_u], [r_pooled[g]])
            mem_attention(kind, T)
            for g in range(4):
                ps, rp = psring.next()
                mm(ps[:, 0:T], [(wpool_sb[:, g, :], pooled[:, g, 0:T])], [r_wpool, r_pooled[g]], [rp])
                dve_stt(sqatt[:, g, 0:T], ps[:, 0:T], pscale_sb[:, g:g + 1], onesT[:, 0:T], ALU.mult, ALU.mult,
                        [rp, r_const], [r_att[g]])
            dve_copy(u_v[:, :, 0:15], u_v[:, :, T:T + 15], [r_u], [r_u])
            if kind == 0 and t == NT0 - 1:
                S.dma("pool", pstate_p.rearrange("(c p) t -> p c t", p=128), u_v[:, :, 0:15], reads=[r_u])
            if kind == 1:
                S.dma("pool", pstate_s.rearrange("(c p) t -> p c t", p=128), u_v[:, :, 0:15], reads=[r_u])
            wgs = [load_w(wout_bf[0], 0, 8, 0, 512, rW[("wout", 0)]), load_w(wout_bf[0], 0, 8, 512, 512, rW[("wout", 0)])]
            cast_some(1)
            for oc in range(8):
                wg, rw = wgs[oc // 4]
                j = oc % 4
                ps, rp = psring.next()
                mm(ps[:, 0:T], [(wg[:, kc, j * 128:(j + 1) * 128], sqatt[:, kc, 0:T]) for kc in range(KC)],
                   r_sqc + [rw], [rp])
                evac_post(oc, ps[:, 0:T], rp, G_MIX_POST, T)
            post_norm_residual(T)
            mlp(0, T)
            yield "mid"
            X.i = xi
            kvf(kind, t, T)
            if kind == 0:
                for c in range(KC):
                    xv = xbufs[X.i][:, c, :].rearrange("p (a r q) -> p a r q", r=2, q=128)
                    xo = xown[:, c, :].rearrange("p (a q) -> p a q", q=128)
                    dve_ts(xo, xv[:, :, 0, :], rflag_sb[:, 1:2], None, ALU.mult, ALU.bypass,
                           [r_xs[X.i][c], r_const], r_a[0:2])
                    dve_stt(xo, xv[:, :, 1, :], rflag_sb[:, 0:1], xo, ALU.mult, ALU.add,
                            [r_xs[X.i][c], r_const] + r_a[0:2], r_a[0:2])
                S.dma("pool", x1own[:, t * 256:(t + 1) * 256].rearrange("(c p) t -> p c t", p=128), xown,
                      reads=r_a[0:2], writes=[r_x1own])
            else:
                S.dma("pool", x1s.rearrange("(c p) t -> p c t", p=128), xbufs[X.i][:, :, 0:T], reads=r_xs[X.i], writes=[r_x1s])

        def layer1_tile(kind, t, T, nxt=None):
            begin_tile((1, kind, t))
            if kind == 0:
                nkb = 8 * t + 8
                kr = 66
                kcols = nkb * 128
            else:
                nkb = NSB + 1
                kr = 64
                kcols = PAST + ST
            prep_xg(xbufs[X.i], r_xs[X.i], G_MIX_PRE + 8, T)
            wg, rw = load_w(win_bf[1], 0, 8, 0, 512, rW[("win", 1)])
            for hd in range(8):
                ps, rp = psring.next()
                mm(ps[0:64, 0:T], [(wg[:, kc, hd * 64:(hd + 1) * 64], hbuf[:, kc, 0:T]) for kc in range(KC)],
                   r_h + [rw], [rp])
                dve_stt(qaug[0:64, hd, 0:T], ps[0:64, 0:T], 0.125, rstd[0:64, 0:T], ALU.mult, ALU.mult,
                        [rp, r_rstd], [r_qaug[hd]])
            wg, rw = load_w(win_bf[1], 0, 8, 512, 512, rW[("win", 1)])
            for hm in range(4):
                ps, rp = psring.next()
                mm(ps[:, 0:T], [(wg[:, kc, hm * 128:(hm + 1) * 128], hbuf[:, kc, 0:T]) for kc in range(KC)],
                   r_h + [rw], [rp])
                dve_tt(qm[:, hm, 0:T], ps[:, 0:T], rstd[:, 0:T], ALU.mult, [rp, r_rstd], [r_qm[hm]])
            prefetch_x(nxt)
            mem_attention(kind, T)
            if kind == 0:
                jl = 4 * t + 3
                pstep = Cb_own[:].ap[0][0]
                cref_b = bass.AP(Cb_own[:].tensor, jl * 8, [[pstep, 128], [0, nkb], [1, 8]])
                dve_tt(biasK[:, 0:nkb, :], cref_b, Call[:, 0:nkb, :], ALU.subtract, [r_Cbo[jl]] + r_Call[0:nkb], [r_biasK])
                for m in range(4):
                    dve_tt(offt[:, m, :], Cb_own[:, 4 * t + m, :], Cb_own[:, jl, :], ALU.subtract,
                           [r_Cbo[4 * t + m], r_Cbo[jl]], [r_offt])
                dve_copy(offhi[:], offt[:], [r_offt], [r_offhi])
                dve_tt(offmid[:], offt[:], offhi[:], ALU.subtract, [r_offt, r_offhi], [r_offmid])
                dve_ts(tmp2[:], offmid[:], pmask_sb[:, 1:2], None, ALU.mult, ALU.bypass, [r_offmid, r_const], [r_tmp2])
                dve_stt(src2[:], offhi[:], pmask_sb[:, 0:1], tmp2[:], ALU.mult, ALU.add, [r_offhi, r_tmp2, r_const],
                        [r_src2])
                for hd in range(8):
                    for m in range(4):
                        dve_stt(qaug[64:66, hd, m * 128:(m + 1) * 128], zrow[64:66, :], src2[64:66, m, hd:hd + 1],
                                zrow[64:66, :], ALU.add, ALU.add, [r_src2, r_zrow], [r_qaug[hd]])
                S.dma("sp", vaug_v[:, 0:nkb, :, :].rearrange("p b h e -> p b (h e)"),
                      vbf[0:nkb * 128, :].rearrange("(b p) e -> p b e", p=128), reads=[r_vbf], writes=r_a)
                ksrc, r_ks = kbf, r_kbf
            else:
                pstep = Cb_s[:].ap[0][0]
                cref_b = bass.AP(Cb_s[:].tensor, 0, [[pstep, 128], [0, nkb], [1, 8]])
                dve_tt(biasK[:, 0:nkb, :], cref_b, Call_s[:, 0:nkb, :], ALU.subtract, [r_Cbs] + r_Calls, [r_biasK])
                S.dma("sp", vaug_v[:, 0:NSB, :, :].rearrange("p b h e -> p b (h e)"),
                      vbf_s[0:PAST, :].rearrange("(b p) e -> p b e", p=128), reads=[r_vbfs], writes=r_a)
                S.dma("sp", vaug_v[0:ST, NSB, :, :].rearrange("p h e -> p (h e)"), vbf_s[PAST:PAST + ST, :],
                      reads=[r_vbfs], writes=r_a)
                ksrc, r_ks = kbf_s, r_kbfs
            pending = [None]

            def finish_head(hd, acc, r_acc):
                act(AF.Ln, lnv[64:65, 0:T], acc[64:65, 0:T], [r_acc], [r_ln])
                act(AF.Exp, rden[64:65, 0:T], lnv[64:65, 0:T], [r_ln], [r_rden], scale=-1.0)
                bc, r_bc = psring.next()
                mm(bc[0:64, 0:T], [(onesf[64:65, 0:64], rden[64:65, 0:T])], [r_const, r_rden], [r_bc])
                act_copy(rstd[0:64, 0:T], bc[0:64, 0:T], [r_bc], [r_rstd])
                dve_tt(attnh[0:64, hd, 0:T], acc[0:64, 0:T], rstd[0:64, 0:T], ALU.mult, [r_acc, r_rstd], [r_attnh[hd]])

            for hd in range(8):
                kh, r_kh = khring.next()
                S.dma("sp", kh[0:64, 0:kcols], ksrc[hd * 64:(hd + 1) * 64, 0:kcols], reads=[r_ks], writes=[r_kh])
                acc, r_acc = accring.next()

                def issue_S(i, hd=hd, kh=kh, r_kh=r_kh):
                    kn = 128 if (kind == 0 or i < NSB) else ST
                    c0 = ((i - 8 * t) // 2) * 128 if (kind == 0 and i >= 8 * t) else 0
                    s_ps, r_s = psring.next()
                    pairs = [(kh[0:kr, i * 128:i * 128 + kn], qaug[0:kr, hd, c0:T])]
                    rd = [r_kh, r_qaug[hd], r_khones]
                    if kind == 0 and i >= 8 * t:
                        pairs.append((ident[:, :], masks_sb[:, i - 8 * t, c0:T]))
                        rd.append(r_const)
                    if kind == 1 and i == NSB:
                        pairs.append((ident[0:kn, 0:kn], mask_s_sb[0:kn, 0:T]))
                        rd.append(r_const)
                    mm(s_ps[0:kn, c0:T], pairs, rd, [r_s])
                    return s_ps, r_s, kn, c0
                DEPTH = 3
                q_s = [issue_S(i) for i in range(min(DEPTH, nkb))]
                if pending[0] is not None:
                    finish_head(*pending[0])
                    pending[0] = None
                for i in range(nkb):
                    s_ps, r_s, kn, c0 = q_s.pop(0)
                    pt, r_pt = ptmp.next()
                    act(AF.Exp, pt[0:kn, c0:T], s_ps[0:kn, c0:T], [r_s, r_biasK], [r_pt],
                        bias=biasK[0:kn, i, hd:hd + 1])
                    if i + DEPTH < nkb:
                        q_s.append(issue_S(i + DEPTH))
                    mm(acc[0:65, c0:T], [(vaug_v[0:kn, i, hd, :], pt[0:kn, c0:T])], r_a + [r_pt], [r_acc],
                       first=(i == 0), last=(i == nkb - 1))
                pending[0] = (hd, acc, r_acc)
            finish_head(*pending[0])
            for half in range(2):
                wmx, r_wmx = wring.next()
                S.dma("sp", wmx[0:64, :, :], wout_bf[1][0:512, half * 512:(half + 1) * 512].rearrange(
                    "(h d) n -> d h n", d=64), reads=rW[("wout", 1)], writes=[r_wmx])
                wme, r_wme = load_w(wout_bf[1], 4, 4, half * 512, 512, rW[("wout", 1)])
                for j in range(4):
                    oc = half * 4 + j
                    ps, rp = psring.next()
                    pairs = [(wmx[0:64, hd, j * 128:(j + 1) * 128], attnh[0:64, hd, 0:T]) for hd in range(8)]
                    pairs += [(wme[:, hm, j * 128:(j + 1) * 128], sqatt[:, 4 + hm, 0:T]) for hm in range(4)]
                    mm(ps[:, 0:T], pairs, r_attnh + r_sqc[4:8] + [r_wmx, r_wme], [rp])
                    evac_post(oc, ps[:, 0:T], rp, G_MIX_POST + 8, T)
            post_norm_residual(T)
            mlp(1, T)
            if kind == 0:
                S.dma("pool", yT_own[:, t * TT:t * TT + T].rearrange("(c p) t -> p c t", p=128), xbufs[X.i][:, :, 0:T],
                      reads=r_xs[X.i])
            else:
                S.dma("pool", ysT.rearrange("(c p) t -> p c t", p=128), xbufs[X.i][:, :, 0:T], reads=r_xs[X.i])

        r_khones = Res("khones")

        if "m" not in SK:
            mem_project(0, sq="sp")
        def sample_cache_prep():
            mem_load_sample(0)
            S.dma("sp", clf_sb[:], clf.rearrange("(b p) h -> p b h", p=128), writes=[r_clf])
            ps, rp = psring.next()
            for b in range(NSB):
                def emit(e, ps=ps, b=b):
                    o = ps[:, b * 8:(b + 1) * 8]
                    e.matmul(o, lhsT=tri_sb[:, :], rhs=clf_sb[:, b, :], start=True, stop=(b == 0))
                    ins = None
                    for b2 in range(b):
                        ins = e.matmul(o, lhsT=onesf[:, :], rhs=clf_sb[:, b2, :], start=False, stop=(b2 == b - 1))
                    return ins if ins is not None else e.matmul(o, lhsT=tri_sb[:, :], rhs=clf_sb[:, b, :],
                                                                start=True, stop=True)
                S.op("pe", emit, [r_clf, r_const], [rp])
            dve_copy(Call_s[:, 0:NSB, :], ps[:, 0:NSB * 8].rearrange("p (b h) -> p b h", h=8), [rp], r_Calls[0:NSB])
            for b in range(NSB if "v" not in SK else 0):
                vft, r_vf = vf.next()
                S.dma("sp", vft[:], cv[b * 128:(b + 1) * 128, :], writes=[r_vf])
                vs, r_vs = vstr.next()
                dve_copy(vs[:, :, 0:64], vft[:].rearrange("p (h e) -> p h e", e=64), [r_vf], [r_vs])
                S.dma("pool", vbf_s[b * 128:(b + 1) * 128, :], vs[:, :, :].rearrange("p h e -> p (h e)"), reads=[r_vs],
                      writes=[r_vbfs])


        if stage >= 1:
            tiles = [layer0_tile(0, t, TT, nxt=((0, 0, t + 1) if t + 1 < NT0 else (0, 1, 0))) for t in range(NT0)]
            tiles.append(layer0_tile(1, 0, ST))

            def step(g):
                try:
                    next(g)
                except StopIteration:
                    pass
            step(tiles[0])
            step(tiles[0])
            for k in range(1, len(tiles)):
                step(tiles[k])
                step(tiles[k - 1])
                step(tiles[k])
            step(tiles[-1])
        if stage >= 2:
            cast_some(len(cast_todo))
            S.dma("pool", kbf_s[:, 0:PAST], ckT, writes=[r_kbfs])
            for kh in khs:
                memset(kh[64:66, :], 1.0, [r_khones, r_u, r_sA, r_sB, khring.items[0][1], khring.items[1][1]]
                       + r_pooled + r_qaug)
            mem_project(1)
            mem_load_sample(1)
            for t in range(NT1):
                layer1_tile(0, t, TT, nxt=((1, 0, t + 1) if t + 1 < NT1 else (1, 1, 0)))
            layer1_tile(1, 0, ST)
        S.finish("pool")
        S.emit_all()
    return nc


_NC_CACHE = {}


def _consts():
    bf = ml_dtypes.bfloat16
    k = np.arange(128)[:, None]
    q = np.arange(128)[None, :]
    diag = np.where(k <= q, 0.0, NEG).astype(np.float32)
    masks = []
    for r in range(2):
        mr = np.zeros((8, 128, TT), np.float32)
        for d in range(8):
            for m in range(4):
                tb = 2 * m + r
                if d < tb:
                    blk = np.zeros((128, 128), np.float32)
                elif d == tb:
                    blk = diag
                else:
                    blk = np.full((128, 128), NEG, np.float32)
                mr[d, :, m * 128:(m + 1) * 128] = blk
        masks.append(mr.astype(bf))
    mask_s = np.full((128, TT), NEG, np.float32)
    mask_s[:, 0:128] = diag
    mask_s = mask_s.astype(bf)
    cntfix = np.ones((128, 4, 16), np.float32)
    for g in range(4):
        w = 2 << g
        for t in range(16):
            cntfix[:, g, t] = w / min(w, t + 1)
    tri = (np.arange(128)[:, None] <= np.arange(128)[None, :]).astype(np.float32)
    lastrow = np.zeros((128, 128), np.float32)
    lastrow[127, :] = 1.0
    lastrow16 = np.zeros((128, 128), np.float32)
    lastrow16[ST - 1, :] = 1.0
    pmask = np.zeros((128, 2), np.float32)
    pmask[64, 0] = 1.0
    pmask[65, 1] = 1.0
    ident = np.eye(128, dtype=np.float32).astype(bf)
    return dict(masks=masks, mask_s=mask_s, cntfix=cntfix, tri=tri, lastrow=lastrow, lastrow16=lastrow16,
                pmask=pmask, ident=ident)


def _col(v):
    v = np.asarray(v, np.float32)
    return np.ascontiguousarray(v.reshape(-1, 128).T)


def kernel(x_prompt, x_sample, cache_pool, cache_k, cache_v, cache_logf, cache_mem_k, cache_mem_v,
           mem_prompt, g_mix_pre, g_mix_post, g_mlp_pre, g_mlp_post, w_in, w_out, w_pool, pool_scale,
           g_kv, w_kvf, b_f, g_mem, w_mem_kv, w_up, w_down, _stage=99):
    f32 = np.float32
    A = lambda a: np.ascontiguousarray(np.asarray(a, f32))
    x_prompt, x_sample = A(x_prompt), A(x_sample)
    cs = _consts()
    gains = np.concatenate([_col(g_mix_pre[0]), _col(g_mix_pre[1]), _col(g_mix_post[0]), _col(g_mix_post[1]),
                            _col(g_mlp_pre[0]), _col(g_mlp_pre[1]), _col(g_mlp_post[0]), _col(g_mlp_post[1]),
                            _col(g_mem[0]), _col(g_mem[1]), _col(g_kv)], axis=1)
    shared = dict(w_in=A(w_in), w_out=A(w_out), w_pool=A(w_pool)[0], w_kvf=A(w_kvf), w_mem=A(w_mem_kv),
                  w_up=A(w_up), w_down=A(w_down), gains=np.ascontiguousarray(gains),
                  pscale=_col(np.asarray(pool_scale)[0]),
                  bfb=np.ascontiguousarray(np.broadcast_to(np.tile(np.asarray(b_f, f32), 4)[None, :], (128, 32))),
                  mask_s=cs["mask_s"], cntfix=cs["cntfix"], tri=cs["tri"], lastrow=cs["lastrow"],
                  lastrow16=cs["lastrow16"], pmask=cs["pmask"], ident=cs["ident"])
    xT_seq = [np.ascontiguousarray(x_prompt[b].T) for b in range(4)]
    memT_seq = [np.ascontiguousarray(np.asarray(mem_prompt, f32)[b].T) for b in range(4)]
    in_maps = []
    for c in range(8):
        b, r = c // 2, c % 2
        m = dict(shared)
        m["xT"] = xT_seq[b]
        m["memT"] = memT_seq[b]
        m["xsT"] = np.ascontiguousarray(x_sample[c].T)
        m["poolprev"] = np.ascontiguousarray(np.asarray(cache_pool, f32)[0, c].T)
        m["ckT"] = np.ascontiguousarray(np.asarray(cache_k, f32)[c].reshape(PAST, 512).T)
        m["cv"] = np.ascontiguousarray(np.asarray(cache_v, f32)[c].reshape(PAST, 512))
        m["clf"] = A(np.asarray(cache_logf)[c])
        m["cmkT"] = np.ascontiguousarray(np.asarray(cache_mem_k, f32)[:, c].reshape(2, 256, 512).transpose(0, 2, 1))
        m["cmv"] = np.ascontiguousarray(np.asarray(cache_mem_v, f32)[:, c].reshape(2, 256, 512))
        m["flag"] = np.full((128, 256), r, np.int32)
        m["rflag"] = np.ascontiguousarray(np.broadcast_to(np.array([[r, 1 - r]], np.float32), (128, 2)))
        m["masks"] = cs["masks"][r]
        in_maps.append(m)
    key = int(_stage)
    if key not in _NC_CACHE:
        _NC_CACHE[key] = build(key)
    nc = _NC_CACHE[key]
    ncores = int(os.environ.get("KCORES", "8"))
    res = run_bass_kernel_spmd(nc, in_maps[:ncores], core_ids=list(range(ncores))).results
    res = [res[c % ncores] for c in range(8)]

    y_prompt = np.empty((4, SEQ, D), f32)
    y_sample = np.empty((8, ST, D), f32)
    ps_p = np.empty((1, 4, 15, 512), f32)
    ps_s = np.empty((1, 8, 15, 512), f32)
    k_p = np.empty((4, SEQ, 8, 64), f32)
    v_p = np.empty((4, SEQ, 8, 64), f32)
    lf_p = np.empty((4, SEQ, 8), f32)
    k_s = np.empty((8, ST, 8, 64), f32)
    v_s = np.empty((8, ST, 8, 64), f32)
    lf_s = np.empty((8, ST, 8), f32)
    mk_p = np.empty((2, 4, 256, 4, 128), f32)
    mv_p = np.empty((2, 4, 256, 4, 128), f32)
    for c in range(8):
        b, r = c // 2, c % 2
        o = res[c]
        yo = o["yT_own"].T.reshape(NBLK // 2, 128, D)
        y_prompt[b].reshape(NBLK // 2, 2, 128, D)[:, r] = yo
        y_sample[c] = o["ysT"].T
        ps_s[0, c] = o["pstate_s"].T
        k_s[c] = o["kT_s"].T.reshape(ST, 8, 64)
        v_s[c] = o["v_s"].reshape(ST, 8, 64)
        lf_s[c] = o["logf_s"]
        if r == 0:
            ps_p[0, b] = o["pstate_p"].T
            k_p[b] = o["kT_p"].T.reshape(SEQ, 8, 64)
            v_p[b] = o["v_p"].reshape(SEQ, 8, 64)
            lf_p[b] = o["logf_p"]
            mk_p[:, b] = o["mkT_p"].transpose(0, 2, 1).reshape(2, 256, 4, 128)
            mv_p[:, b] = o["mv_p"].reshape(2, 256, 4, 128)
    return (y_prompt, y_sample, ps_p, ps_s, k_p, v_p, lf_p, k_s, v_s, lf_s, mk_p, mv_p)
```
